# Optimizing a Trainium2 kernel written in Bass

```python
import jax, jax.numpy as jnp
from jax import lax
import numpy as np

D_MODEL = 2048
BATCH = 4
SEQ = 2048
DEPTH = 1
DEC_BATCH = 128
DEC_SEQ = 8
PAST_LEN = 16384
PAGE_SIZE = 128

D_RNN = D_MODEL
N_LRU_BLOCKS = 8
LRU_BLOCK = D_RNN // N_LRU_BLOCKS
CONV_W = 4
LRU_C = 8.0
N_RET_HEADS = 8
RET_DK = D_MODEL // N_RET_HEADS
RET_V_EXPAND = 2
RET_DV = RET_V_EXPAND * D_MODEL // N_RET_HEADS
QK_DIM = N_RET_HEADS * RET_DK
V_DIM = N_RET_HEADS * RET_DV
RET_CHUNK = 64
ROPE_BASE = 10000.0
D_FF = 5632
PLE_DIM = 256
EPS = 1e-6
IN_COLS = 2 * D_RNN + 2 * QK_DIM + 2 * V_DIM + 2 * D_MODEL

kernel_name = "hybrid_rglru_retention_macaron_step"


def rmsnorm(x, g):
    x32 = x.astype(jnp.float32)
    return x32 * lax.rsqrt(jnp.mean(x32 * x32, axis=-1, keepdims=True) + EPS) * g


def swiglu(u, wg, wu, wd):
    return jnp.einsum('btf,fd->btd', jax.nn.silu(jnp.einsum('btd,df->btf', u, wg)) * jnp.einsum('btd,df->btf', u, wu), wd)


def causal_conv(xb, buf, w, b):
    T = xb.shape[1]
    xp = jnp.concatenate([buf.astype(jnp.float32), xb], axis=1)
    y = b + sum(xp[:, j:j + T] * w[j] for j in range(CONV_W))
    return y, xp[:, T:]


def _lin_combine(e1, e2):
    a1, b1 = e1
    a2, b2 = e2
    return a1 * a2, a2 * b1 + b2


def rg_lru(xc, h0, wa, ba, wx, bx, lam, reset_first):
    B, T, _ = xc.shape
    xb = xc.reshape(B, T, N_LRU_BLOCKS, LRU_BLOCK)
    r = jax.nn.sigmoid(jnp.einsum('btnc,ncd->btnd', xb, wa).reshape(B, T, D_RNN) + ba)
    gi = jax.nn.sigmoid(jnp.einsum('btnc,ncd->btnd', xb, wx).reshape(B, T, D_RNN) + bx)
    log_a = -LRU_C * r * jax.nn.softplus(-lam.astype(jnp.float32))
    a = jnp.exp(log_a)
    mult = jnp.sqrt(-jnp.expm1(2.0 * log_a))
    if reset_first:
        mult = mult.at[:, 0].set(1.0)
    bterm = mult * (gi * xc)
    bterm = bterm.at[:, 0].add(a[:, 0] * h0.astype(jnp.float32))
    _, h = lax.associative_scan(_lin_combine, (a, bterm), axis=1)
    return h, h[:, -1]


def rope(x, pos):
    half = x.shape[-1] // 2
    inv = ROPE_BASE ** (-jnp.arange(half, dtype=jnp.float32) / half)
    ang = pos.astype(jnp.float32)[:, None] * inv
    cos = jnp.cos(ang)[None, :, None, :]
    sin = jnp.sin(ang)[None, :, None, :]
    x1, x2 = x[..., :half], x[..., half:]
    return jnp.concatenate([x1 * cos - x2 * sin, x2 * cos + x1 * sin], axis=-1)


def retention(q, k, v, s0):
    B, T, H, _ = q.shape
    chunk = RET_CHUNK if T % RET_CHUNK == 0 else T
    nc = T // chunk

    def to_chunks(a):
        return a.reshape(B, nc, chunk, H, a.shape[-1]).transpose(1, 0, 3, 2, 4)

    log_g = jnp.log1p(-jnp.exp2(-5.0 - jnp.arange(H, dtype=jnp.float32)))
    idx = jnp.arange(chunk, dtype=jnp.float32)
    diff = idx[:, None] - idx[None, :]
    dmask = jnp.where(diff >= 0, jnp.exp(jnp.maximum(diff, 0.0)[None] * log_g[:, None, None]), 0.0)
    cross_decay = jnp.exp((idx[None] + 1.0) * log_g[:, None])[..., None]
    state_decay = jnp.exp((chunk - 1.0 - idx[None]) * log_g[:, None])[..., None]
    chunk_decay = jnp.exp(chunk * log_g)[:, None, None]

    def step(S, inp):
        qc, kc, vc = inp
        scores = jnp.einsum('bhik,bhjk->bhij', qc, kc) * dmask
        inner = jnp.einsum('bhij,bhjv->bhiv', scores, vc)
        cross = jnp.einsum('bhik,bhkv->bhiv', qc, S) * cross_decay
        S_new = S * chunk_decay + jnp.einsum('bhjk,bhjv->bhkv', kc * state_decay, vc)
        return S_new, inner + cross

    s_last, outs = lax.scan(step, s0.astype(jnp.float32), (to_chunks(q), to_chunks(k), to_chunks(v)))
    o = outs.transpose(1, 0, 3, 2, 4).reshape(B, T, H, v.shape[-1])
    return o, s_last


def decoder_layer(x, pe, h0, conv_buf, s0, pos0, reset_first, lp):
    B, T, _ = x.shape
    x = x + 0.5 * swiglu(rmsnorm(x, lp['ffn1_norm']), lp['ffn1_wg'], lp['ffn1_wu'], lp['ffn1_wd'])
    u = rmsnorm(x, lp['mix_norm'])
    z = jnp.einsum('btd,dc->btc', u, lp['w_in']) + lp['b_in']
    cuts = [D_RNN, 2 * D_RNN, 2 * D_RNN + QK_DIM, 2 * D_RNN + 2 * QK_DIM, 2 * D_RNN + 2 * QK_DIM + V_DIM,
            2 * D_RNN + 2 * QK_DIM + 2 * V_DIM, 2 * D_RNN + 2 * QK_DIM + 2 * V_DIM + D_MODEL]
    xa, ga, q, k, v, gr, gate_a, gate_b = jnp.split(z, cuts, axis=-1)
    xc, conv_new = causal_conv(xa, conv_buf, lp['conv_w'], lp['conv_b'])
    ha, h_last = rg_lru(xc, h0, lp['lru_wa'], lp['lru_ba'], lp['lru_wx'], lp['lru_bx'], lp['lru_lambda'], reset_first)
    oa = ha * jax.nn.gelu(ga)
    pos = pos0 + jnp.arange(T)
    qh = rope(q.reshape(B, T, N_RET_HEADS, RET_DK), pos)
    kh = rope(k.reshape(B, T, N_RET_HEADS, RET_DK), pos) * (RET_DK ** -0.5)
    vh = v.reshape(B, T, N_RET_HEADS, RET_DV)
    ob, s_new = retention(qh, kh, vh, s0)
    ob = rmsnorm(ob, lp['ret_norm']).reshape(B, T, V_DIM) * jax.nn.silu(gr)
    merged = (jax.nn.sigmoid(gate_a) * jnp.einsum('btc,cd->btd', oa, lp['proj_a'])
              + jax.nn.sigmoid(gate_b) * jnp.einsum('btc,cd->btd', ob, lp['proj_b']))
    x = x + jnp.einsum('btd,de->bte', merged, lp['w_out'])
    x = x + 0.5 * swiglu(rmsnorm(x, lp['ffn2_norm']), lp['ffn2_wg'], lp['ffn2_wu'], lp['ffn2_wd'])
    gate = jax.nn.sigmoid(jnp.einsum('btd,de->bte', rmsnorm(x, lp['ple_norm']), lp['ple_wg']) + lp['ple_bg'])
    x = x + gate * jnp.einsum('btp,pd->btd', pe.astype(jnp.float32), lp['ple_proj'])
    return x, h_last, conv_new, s_new


def setup_inputs(seed: int = 0) -> dict:
    key = jax.random.key(seed)
    ks = jax.random.split(key, 40)
    f32 = jnp.float32

    def w(k, shape, fan_in, scale=1.0):
        return jax.random.normal(k, shape, f32) * (scale * fan_in ** -0.5)

    def gain(k, shape):
        return 1.0 + 0.05 * jax.random.normal(k, shape, f32)

    def bias(k, shape):
        return 0.02 * jax.random.normal(k, shape, f32)

    a0 = jax.random.uniform(ks[20], (DEPTH, D_RNN), f32, minval=0.9, maxval=0.999)
    s = a0 ** (1.0 / LRU_C)
    lam = jnp.log(s) - jnp.log1p(-s)
    return {
        'x_prompt': jax.random.normal(ks[0], (BATCH, SEQ, D_MODEL), f32),
        'x_sample': jax.random.normal(ks[1], (DEC_BATCH, DEC_SEQ, D_MODEL), f32),
        'p_prompt': jax.random.normal(ks[2], (DEPTH, BATCH, SEQ, PLE_DIM), f32),
        'p_sample': jax.random.normal(ks[3], (DEPTH, DEC_BATCH, DEC_SEQ, PLE_DIM), f32),
        'state_lru': 0.5 * jax.random.normal(ks[4], (DEPTH, DEC_BATCH, D_RNN), f32),
        'state_conv': jax.random.normal(ks[5], (DEPTH, DEC_BATCH, CONV_W - 1, D_RNN), f32),
        'state_ret': 0.5 * jax.random.normal(ks[6], (DEPTH, DEC_BATCH, N_RET_HEADS, RET_DK, RET_DV), f32),
        'ffn1_norm': gain(ks[7], (DEPTH, D_MODEL)),
        'ffn1_wg': w(ks[8], (DEPTH, D_MODEL, D_FF), D_MODEL),
        'ffn1_wu': w(ks[9], (DEPTH, D_MODEL, D_FF), D_MODEL),
        'ffn1_wd': w(ks[10], (DEPTH, D_FF, D_MODEL), D_FF, 0.5),
        'mix_norm': gain(ks[11], (DEPTH, D_MODEL)),
        'w_in': w(ks[12], (DEPTH, D_MODEL, IN_COLS), D_MODEL),
        'b_in': bias(ks[13], (DEPTH, IN_COLS)),
        'conv_w': w(ks[14], (DEPTH, CONV_W, D_RNN), CONV_W),
        'conv_b': bias(ks[15], (DEPTH, D_RNN)),
        'lru_wa': w(ks[16], (DEPTH, N_LRU_BLOCKS, LRU_BLOCK, LRU_BLOCK), LRU_BLOCK),
        'lru_ba': bias(ks[17], (DEPTH, D_RNN)),
        'lru_wx': w(ks[18], (DEPTH, N_LRU_BLOCKS, LRU_BLOCK, LRU_BLOCK), LRU_BLOCK),
        'lru_bx': bias(ks[19], (DEPTH, D_RNN)),
        'lru_lambda': lam,
        'ret_norm': gain(ks[21], (DEPTH, N_RET_HEADS, RET_DV)),
        'proj_a': w(ks[22], (DEPTH, D_RNN, D_MODEL), D_RNN),
        'proj_b': w(ks[23], (DEPTH, V_DIM, D_MODEL), V_DIM),
        'w_out': w(ks[24], (DEPTH, D_MODEL, D_MODEL), D_MODEL, 0.5),
        'ffn2_norm': gain(ks[25], (DEPTH, D_MODEL)),
        'ffn2_wg': w(ks[26], (DEPTH, D_MODEL, D_FF), D_MODEL),
        'ffn2_wu': w(ks[27], (DEPTH, D_MODEL, D_FF), D_MODEL),
        'ffn2_wd': w(ks[28], (DEPTH, D_FF, D_MODEL), D_FF, 0.5),
        'ple_norm': gain(ks[29], (DEPTH, D_MODEL)),
        'ple_wg': w(ks[30], (DEPTH, D_MODEL, D_MODEL), D_MODEL),
        'ple_bg': bias(ks[31], (DEPTH, D_MODEL)),
        'ple_proj': w(ks[32], (DEPTH, PLE_DIM, D_MODEL), PLE_DIM, 0.5),
        'final_norm': gain(ks[33], (D_MODEL,)),
    }


def reference(x_prompt, x_sample, p_prompt, p_sample, state_lru, state_conv, state_ret,
              ffn1_norm, ffn1_wg, ffn1_wu, ffn1_wd, mix_norm, w_in, b_in, conv_w, conv_b,
              lru_wa, lru_ba, lru_wx, lru_bx, lru_lambda, ret_norm, proj_a, proj_b, w_out,
              ffn2_norm, ffn2_wg, ffn2_wu, ffn2_wd, ple_norm, ple_wg, ple_bg, ple_proj, final_norm):
    xp = x_prompt.astype(jnp.float32)
    xs = x_sample.astype(jnp.float32)
    bp = x_prompt.shape[0]
    lru_p, conv_p, ret_p, lru_s, conv_s, ret_s = [], [], [], [], [], []
    for i in range(DEPTH):
        lp = {
            'ffn1_norm': ffn1_norm[i], 'ffn1_wg': ffn1_wg[i], 'ffn1_wu': ffn1_wu[i], 'ffn1_wd': ffn1_wd[i],
            'mix_norm': mix_norm[i], 'w_in': w_in[i], 'b_in': b_in[i], 'conv_w': conv_w[i], 'conv_b': conv_b[i],
            'lru_wa': lru_wa[i], 'lru_ba': lru_ba[i], 'lru_wx': lru_wx[i], 'lru_bx': lru_bx[i],
            'lru_lambda': lru_lambda[i], 'ret_norm': ret_norm[i], 'proj_a': proj_a[i], 'proj_b': proj_b[i],
            'w_out': w_out[i], 'ffn2_norm': ffn2_norm[i], 'ffn2_wg': ffn2_wg[i], 'ffn2_wu': ffn2_wu[i],
            'ffn2_wd': ffn2_wd[i], 'ple_norm': ple_norm[i], 'ple_wg': ple_wg[i], 'ple_bg': ple_bg[i],
            'ple_proj': ple_proj[i],
        }
        xp, hp, cp, sp = decoder_layer(
            xp, p_prompt[i], jnp.zeros((bp, D_RNN), jnp.float32),
            jnp.zeros((bp, CONV_W - 1, D_RNN), jnp.float32),
            jnp.zeros((bp, N_RET_HEADS, RET_DK, RET_DV), jnp.float32), 0, True, lp)
        xs, hs, cs, ss = decoder_layer(xs, p_sample[i], state_lru[i], state_conv[i], state_ret[i], PAST_LEN, False, lp)
        lru_p.append(hp); conv_p.append(cp); ret_p.append(sp)
        lru_s.append(hs); conv_s.append(cs); ret_s.append(ss)
    y_prompt = rmsnorm(xp, final_norm).astype(x_prompt.dtype)
    y_sample = rmsnorm(xs, final_norm).astype(x_sample.dtype)
    new_lru_prompt = jnp.stack(lru_p).astype(state_lru.dtype)
    new_conv_prompt = jnp.stack(conv_p).astype(state_conv.dtype)
    new_ret_prompt = jnp.stack(ret_p).astype(state_ret.dtype)
    new_lru_sample = jnp.stack(lru_s).astype(state_lru.dtype)
    new_conv_sample = jnp.stack(conv_s).astype(state_conv.dtype)
    new_ret_sample = jnp.stack(ret_s).astype(state_ret.dtype)
    return (y_prompt, y_sample, new_lru_prompt, new_conv_prompt, new_ret_prompt, new_lru_sample, new_conv_sample, new_ret_sample)
```

```python
import numpy as np
import concourse.bass as bass
import concourse.mybir as mybir
from concourse.bass_utils import run_bass_kernel_spmd

F32 = mybir.dt.float32
BF16 = mybir.dt.bfloat16
AF = mybir.ActivationFunctionType
ALU = mybir.AluOpType

NCORES = 8
DM = 2048
DFF = 5632
NH = 8
EPS = 1e-6
TP = 512
TS = 128
NPT = 1024 // TP

V_FFN1, V_MIX, V_FFN2, V_PLE, V_FIN, V_BIN = 0, 16, 32, 48, 64, 80
V_CW, V_CB, V_BA, V_BX, V_LAM, V_RN, V_PBG = 240, 304, 320, 336, 352, 368, 400
NV = 416
C_XA, C_GA, C_Q, C_K, C_V, C_GR, C_GTA, C_GTB = 0, 2048, 4096, 6144, 8192, 12288, 16384, 18432

SAME_ENGINE_SYNC = True


class Buf:
    __slots__ = ("name", "lastw", "readers")

    def __init__(self, name):
        self.name = name
        self.lastw = None
        self.readers = {}


class Sched:
    def __init__(self, nc, es):
        self.nc = nc
        self.eng = {"pe": nc.tensor, "act": nc.scalar, "dve": nc.vector, "pool": nc.gpsimd, "sp": nc.sync}
        self.sem = {}
        self.cnt = {}
        self.seen = {k: {} for k in self.eng}
        for k in ("pe", "act", "dve"):
            self.sem[k] = es.enter_context(nc.semaphore("sem_" + k))
            self.cnt[k] = 0
        self.dsems = {"sp": [], "pool": []}
        for q, n in (("sp", 16), ("pool", 8)):
            for i in range(n):
                key = "d_%s_%d" % (q, i)
                self.sem[key] = es.enter_context(nc.semaphore(key))
                self.cnt[key] = 0
                self.dsems[q].append(key)
        self.drr = {"sp": 0, "pool": 0}
        self.out_tags = []

    def _wait(self, e, deps):
        for k, v in deps.items():
            if k == e and (e == "pe" or not SAME_ENGINE_SYNC):
                continue
            if self.seen[e].get(k, 0) < v:
                self.eng[e].wait_ge(self.sem[k], v)
                self.seen[e][k] = v

    @staticmethod
    def _add(deps, tag):
        if tag is not None:
            k, v = tag
            if deps.get(k, 0) < v:
                deps[k] = v

    def _deps(self, reads, writes):
        deps = {}
        for b in reads:
            self._add(deps, b.lastw)
        for b in writes:
            self._add(deps, b.lastw)
            for k, v in b.readers.items():
                self._add(deps, (k, v))
        return deps

    def _commit(self, tag, reads, writes):
        k, v = tag
        for b in reads:
            if b.readers.get(k, 0) < v:
                b.readers[k] = v
        for b in writes:
            b.lastw = tag
            b.readers = {}

    def op(self, e, fn, reads=(), writes=()):
        self._wait(e, self._deps(reads, writes))
        ins = fn(self.eng[e])
        self.cnt[e] += 1
        ins.then_inc(self.sem[e], 1)
        self._commit((e, self.cnt[e]), reads, writes)

    def dma(self, q, out, in_, reads=(), writes=(), is_out=False):
        key = self.dsems[q][self.drr[q] % len(self.dsems[q])]
        self.drr[q] += 1
        deps = self._deps(reads, writes)
        if self.cnt[key] > 0:
            self._add(deps, (key, self.cnt[key]))
        self._wait(q, deps)
        self.cnt[key] += 16
        self.eng[q].dma_start(out=out, in_=in_).then_inc(self.sem[key], 16)
        tag = (key, self.cnt[key])
        self._commit(tag, reads, writes)
        if is_out:
            self.out_tags.append(tag)

    def finish(self):
        deps = {}
        for t in self.out_tags:
            self._add(deps, t)
        self._wait("sp", deps)


def build_nc():
    from contextlib import ExitStack
    nc = bass.Bass("TRN2", target_bir_lowering=False)
    es = ExitStack()

    def DI(name, shape):
        return nc.dram_tensor(name, shape, F32, kind="ExternalInput").ap()

    def DO(name, shape):
        return nc.dram_tensor(name, shape, F32, kind="ExternalOutput").ap()

    xp = DI("xp", [1024, DM]); xq = DI("xq", [1024, DM]); xs = DI("xs", [TS, DM])
    pp = DI("pp", [1024, 256]); ps = DI("ps", [TS, 256])
    slru = DI("slru", [16, DM]); sconv = DI("sconv", [48, DM]); sret = DI("sret", [16 * NH * 256, 512])
    W = {}
    for nm, shp in (("ffn1_wg", [DM, DFF]), ("ffn1_wu", [DM, DFF]), ("ffn1_wd", [DFF, DM]),
                    ("w_in", [DM, 20480]), ("lru_wa", [NH * 256, 256]), ("lru_wx", [NH * 256, 256]),
                    ("proj_a", [DM, DM]), ("proj_b", [4096, DM]), ("w_out", [DM, DM]),
                    ("ffn2_wg", [DM, DFF]), ("ffn2_wu", [DM, DFF]), ("ffn2_wd", [DFF, DM]),
                    ("ple_wg", [DM, DM]), ("ple_proj", [256, DM])):
        W[nm] = DI(nm, shp)
    vecs_d = DI("vecs", [128, NV]); bv_d = DI("bv", [1, 4096])
    identf_d = DI("identf", [128, 128])
    cosp_d = DI("cosp", [128, 1024]); sinp_d = DI("sinp", [128, 1024])
    cosq_d = DI("cosq", [128, 1024]); sinq_d = DI("sinq", [128, 1024])
    flag_d = DI("flag", [128, 2])
    coss_d = DI("coss", [128, 128]); sins_d = DI("sins", [128, 128])
    mkp_d = DI("mkp", [NH * 128, 128]); mks_d = DI("mks", [NH * 128, 128])
    cdp_d = DI("cdp", [NH * 128, 512]); cds_d = DI("cds", [NH * 128, 128])
    sdec_d = DI("sdec", [128, 16]); seqm_d = DI("seqm", [128, 16])

    yp = DO("yp", [1024, DM]); ys = DO("ys", [TS, DM])
    lrup = DO("lrup", [16, 128]); convp = DO("convp", [3, DM]); retp = DO("retp", [NH * 256, 512])
    lrus = DO("lrus", [16, DM]); convs = DO("convs", [48, DM]); rets = DO("rets", [16 * NH * 256, 512])

    def SB(name, shape, dt=F32):
        return es.enter_context(nc.sbuf_tensor(name, shape, dt))

    x_fm = SB("x_fm", [128, 16, TP]); xB = [Buf("x%d" % i) for i in range(16)]
    u_bf = SB("u_bf", [128, 16, TP], BF16); uB = [Buf("u%d" % i) for i in range(16)]
    scr = SB("scr", [128, 48, TP], BF16); sB = [Buf("s%d" % i) for i in range(48)]
    NSB = 4
    S_buf = SB("S_buf", [128, NSB, 2, 512]); SbB = [Buf("Sbuf%d" % i) for i in range(NSB)]
    sscr = nc.dram_tensor("sscr", [NH * 256, 512], F32, kind="Internal").ap()
    sscrB = [Buf("sscr%d" % h) for h in range(NH)]
    NSLOT = 5
    wring = SB("wring", [128, NSLOT, 16, 256], BF16); wB = [Buf("w%d" % i) for i in range(NSLOT)]
    NA = 14
    arena = SB("arena", [128, NA, 512]); aB = [Buf("a%d" % i) for i in range(NA)]
    arena_b = arena.bitcast(BF16)
    xe = SB("xe", [128, 2, 516]); xeB = [Buf("xe0"), Buf("xe1")]
    vecs = SB("vecs_sb", [128, NV]); vB = Buf("vecs")
    cl = SB("cl", [128, 16]); clB = Buf("cl")
    halo = SB("halo", [128, 16, 3]); haloB = Buf("halo")
    hprev = SB("hprev", [128, 16]); hpB = Buf("hprev")
    h0fm = SB("h0fm", [128, 16, 16]); h0B = Buf("h0fm")
    identf = SB("identf_sb", [128, 128]); idB = Buf("identf")
    identb = SB("identb", [128, 128], BF16); idbB = Buf("identb")
    onesb = SB("onesb", [128, 128], BF16); onB = Buf("onesb")
    bvb = SB("bvb", [1, 512], BF16); bvB = Buf("bvb")
    bvf = SB("bvf", [1, 512]); bvfB = Buf("bvf")
    cos_sb = SB("cos_sb", [128, TP]); sin_sb = SB("sin_sb", [128, TP]); csB = Buf("cossin")
    mk_sb = SB("mk_sb", [128, 2, 128]); mkB = [Buf("mk0"), Buf("mk1")]
    cd_sb = SB("cd_sb", [128, 1, TP]); cdB = [Buf("cd0"), Buf("cd0b")]
    sdec = SB("sdec_sb", [128, 16]); sdB = Buf("sdec")
    flag = SB("flag_sb", [128, 2]); flB = Buf("flag")
    seqm = SB("seqm_sb", [128, 16]); sqB = Buf("seqm")

    banks = [es.enter_context(nc.psum_tensor("bank%d" % i, [128, 512], F32)) for i in range(7)]
    bankB = [Buf("bank%d" % i) for i in range(7)]
    bank7 = es.enter_context(nc.psum_tensor("bank7", [128, 1024], BF16)); b7B = Buf("bank7")

    S = Sched(nc, es)
    st = {"bank": 0, "slot": 0, "alt": 0, "sbuf": 0}

    def nbank(avoid=None):
        i = st["bank"] % 7
        st["bank"] += 1
        if avoid is not None and banks[i] is avoid:
            i = st["bank"] % 7
            st["bank"] += 1
        return banks[i], bankB[i]

    def nsbuf():
        i = st["sbuf"] % NSB
        st["sbuf"] += 1
        return S_buf[:, i], SbB[i]

    def alt():
        st["alt"] += 1
        return "act" if st["alt"] % 2 else "dve"

    def A(i, n=1):
        return arena[:, i:i + n, :]

    def Ab(i):
        return arena_b[:, i, :]

    def slab(wd, r0, nk, c0, ncols=256):
        i = st["slot"] % NSLOT
        st["slot"] += 1
        src = wd[r0:r0 + nk * 128, c0:c0 + ncols].rearrange("(kc p) n -> p kc n", p=128)
        S.dma("pool", wring[:, i, 0:nk, 0:ncols], src, writes=[wB[i]])
        return wring[:, i], wB[i]

    def mm_group(outap, outB, pairs, rbufs, start=True, stop=True):
        n = len(pairs)

        def fn(pe):
            ins = None
            for j, (l, r) in enumerate(pairs):
                ins = pe.matmul(outap, l, r, start=(start and j == 0), stop=(stop and j == n - 1))
            return ins
        S.op("pe", fn, reads=rbufs, writes=[outB])

    def transposes(outs, ins_, ident, identB, rbufs, outB):
        def fn(pe):
            ins = None
            for o, i in zip(outs, ins_):
                k = i.shape[0]
                ins = pe.transpose(o, i, ident[0:k, 0:k])
            return ins
        S.op("pe", fn, reads=list(rbufs) + [identB], writes=[outB])

    S.dma("sp", vecs[:], vecs_d[:, :], writes=[vB])
    S.dma("sp", identf[:], identf_d[:, :], writes=[idB])
    S.dma("sp", sdec[:], sdec_d[:, :], writes=[sdB])
    S.dma("sp", flag[:], flag_d[:, :], writes=[flB])
    S.dma("sp", seqm[:], seqm_d[:, :], writes=[sqB])
    S.op("dve", lambda e: e.tensor_copy(out=identb[:], in_=identf[:]), reads=[idB], writes=[idbB])
    S.op("dve", lambda e: e.memset(onesb[:], 1.0), writes=[onB])
    S.op("dve", lambda e: e.memset(halo[:], 0.0), writes=[haloB])
    S.op("dve", lambda e: e.memset(hprev[:], 0.0), writes=[hpB])
    S.op("act", lambda e: e.activation(out=cl[:], in_=vecs[:, V_LAM:V_LAM + 16], func=AF.Exp, scale=-1.0),
         reads=[vB], writes=[clB])
    S.op("act", lambda e: e.activation(out=cl[:], in_=cl[:], func=AF.Ln, bias=1.0, scale=1.0),
         reads=[clB], writes=[clB])
    S.op("dve", lambda e: e.tensor_scalar_mul(out=cl[:], in0=cl[:], scalar1=-8.0), reads=[clB], writes=[clB])

    def vcol(c):
        return vecs[:, c:c + 1]

    def load_x(srcs):
        for tc, src in enumerate(srcs):
            a0 = 4 * (tc % 2)
            xin = arena[:, a0:a0 + 4, :]
            xinB = aB[a0:a0 + 4]
            S.dma("sp", xin, src.rearrange("t (a b) -> t a b", a=4), writes=xinB)
            for g in range(4):
                bk, bB = nbank()
                transposes([bk[:, j * 128:(j + 1) * 128] for j in range(4)],
                           [arena[:, a0 + g, j * 128:(j + 1) * 128] for j in range(4)],
                           identf, idB, [xinB[g]], bB)
                e = alt()
                dst = x_fm[:, 4 * g:4 * g + 4, tc * 128:(tc + 1) * 128]
                srcv = bk[:, :].rearrange("p (j t) -> p j t", j=4)
                if e == "act":
                    S.op("act", lambda en, d=dst, s_=srcv: en.copy(out=d, in_=s_), reads=[bB], writes=xB[4 * g:4 * g + 4])
                else:
                    S.op("dve", lambda en, d=dst, s_=srcv: en.tensor_copy(out=d, in_=s_), reads=[bB], writes=xB[4 * g:4 * g + 4])

    def rstd_of(src_fn, nchunks, srcBs, T, inv_n, sq_slot=12, rs_slot=13):
        bk, bB = nbank()
        for kc in range(nchunks):
            half = kc % 2
            sq = arena_b[:, sq_slot, half * 512:half * 512 + T]
            S.op("act", lambda en, o=sq, i=src_fn(kc): en.activation(out=o, in_=i, func=AF.Square),
                 reads=[srcBs[kc]], writes=[aB[sq_slot]])
            mm_group(bk[:, 0:T], bB, [(onesb[:, :], sq)], [onB, aB[sq_slot]], start=(kc == 0), stop=(kc == nchunks - 1))
        rs = arena[:, rs_slot, 0:T]
        S.op("act", lambda en: en.activation(out=rs, in_=bk[:, 0:T], func=AF.Sqrt, scale=inv_n, bias=EPS),
             reads=[bB], writes=[aB[rs_slot]])
        S.op("dve", lambda en: en.reciprocal(out=rs, in_=rs), reads=[aB[rs_slot]], writes=[aB[rs_slot]])
        return rs, aB[rs_slot]

    def rmsnorm_u(gcol, T):
        rs, rsB = rstd_of(lambda kc: x_fm[:, kc, 0:T], 16, xB, T, 1.0 / DM)
        for kc in range(16):
            S.op("dve", lambda en, kc=kc: en.scalar_tensor_tensor(
                out=u_bf[:, kc, 0:T], in0=x_fm[:, kc, 0:T], scalar=vcol(gcol + kc), in1=rs,
                op0=ALU.mult, op1=ALU.mult), reads=[xB[kc], rsB, vB], writes=[uB[kc]])

    def fm_proj(wd, c0, rhs_fn, rhsBs, nk_total, T, consume, r0=0):
        nsl = (nk_total + 15) // 16
        bks = [nbank() for _ in range(2)]
        for s_ in range(nsl):
            k0 = s_ * 16
            nk = min(16, nk_total - k0)
            sl, slB = slab(wd, r0 + k0 * 128, nk, c0)
            for j in range(2):
                bk, bB = bks[j]
                mm_group(bk[:, 0:T], bB,
                         [(sl[:, kc, j * 128:(j + 1) * 128], rhs_fn(k0 + kc)) for kc in range(nk)],
                         [slB] + [rhsBs[k0 + kc] for kc in range(nk)],
                         start=(s_ == 0), stop=(s_ == nsl - 1))
        for j in range(2):
            consume(j, bks[j][0], bks[j][1])

    def u_rhs(T):
        return lambda kc: u_bf[:, kc, 0:T]

    def ffn(wg, wu, wdn, gcol, T):
        rmsnorm_u(gcol, T)
        for hp in range(DFF // 256):
            gb = {}

            def cg(j, bk, bB):
                gb[j] = (bk, bB)
            fm_proj(W[wg], hp * 256, u_rhs(T), uB, 16, T, cg)
            ub = {}

            def cu(j, bk, bB):
                ub[j] = (bk, bB)
            fm_proj(W[wu], hp * 256, u_rhs(T), uB, 16, T, cu)
            for j in range(2):
                hc = hp * 2 + j
                sl_ = 10 + (hc % 2)
                sg = arena[:, sl_, 0:T]
                S.op("act", lambda en, o=sg, i=gb[j][0][:, 0:T]: en.activation(out=o, in_=i, func=AF.Silu),
                     reads=[gb[j][1]], writes=[aB[sl_]])
                S.op("dve", lambda en, o=scr[:, hc, 0:T], a=sg, b=ub[j][0][:, 0:T]: en.tensor_tensor(
                    out=o, in0=a, in1=b, op=ALU.mult), reads=[aB[sl_], ub[j][1]], writes=[sB[hc]])
        for og in range(DM // 256):
            def cd_(j, bk, bB, og=og):
                oc = og * 2 + j
                S.op("dve", lambda en: en.scalar_tensor_tensor(
                    out=x_fm[:, oc, 0:T], in0=bk[:, 0:T], scalar=0.5, in1=x_fm[:, oc, 0:T],
                    op0=ALU.mult, op1=ALU.add), reads=[bB, xB[oc]], writes=[xB[oc]])
            fm_proj(W[wdn], og * 256, lambda kc: scr[:, kc, 0:T], sB, DFF // 128, T, cd_)

    def evac_bias(dst, bk_ap, bcol, rB, wBs, eng="act"):
        if eng == "act":
            S.op("act", lambda en: en.activation(out=dst, in_=bk_ap, func=AF.Identity, bias=vcol(bcol), scale=1.0),
                 reads=[rB, vB], writes=wBs)
        else:
            S.op("dve", lambda en: en.tensor_scalar(out=dst, in0=bk_ap, scalar1=vcol(bcol), scalar2=None,
                                                     op0=ALU.add), reads=[rB, vB], writes=wBs)

    XS = 336

    def mixer(nP, hasS, state_only=False, first_mode=None, last=False, cs=None, s_mode=None):
        TPc = nP * 128
        so = TPc
        T = TPc + (128 if hasS else 0)
        ntc = T // 128
        NS = 16
        assert (not hasS) or 3 + TPc <= XS
        rmsnorm_u(V_MIX, T)
        if hasS:
            cin_flat = arena[:, 12:14, :].rearrange("p a b -> p (a b)")
            for g in range(4):
                S.dma("sp", arena[0:48, 11, :], sconv[:, g * 512:(g + 1) * 512], writes=[aB[11]])
                bk, bB = nbank()
                transposes([bk[:, j * 48:(j + 1) * 48] for j in range(4)],
                           [arena[0:48, 11, j * 128:(j + 1) * 128] for j in range(4)], identf, idB, [aB[11]], bB)
                S.op("dve", lambda en, g=g, bk=bk: en.tensor_copy(out=cin_flat[:, g * 192:(g + 1) * 192], in_=bk[:, 0:192]),
                     reads=[bB], writes=[aB[12], aB[13]])
            cin = cin_flat[:, 0:768].rearrange("p (c s r) -> p c s r", c=16, s=16)
            for g in range(4):
                S.dma("sp", arena[0:16, 11, :], slru[:, g * 512:(g + 1) * 512], writes=[aB[11]])
                bk, bB = nbank()
                transposes([bk[:, j * 16:(j + 1) * 16] for j in range(4)],
                           [arena[0:16, 11, j * 128:(j + 1) * 128] for j in range(4)], identf, idB, [aB[11]], bB)
                S.op("dve", lambda en, g=g, bk=bk: en.tensor_copy(
                    out=h0fm[:, 4 * g:4 * g + 4, :], in_=bk[:, 0:64].rearrange("p (c s) -> p c s", c=4)),
                    reads=[bB], writes=[h0B])

        def xes(j):
            return xe[:, j, XS:XS + 176].rearrange("p (s t) -> p s t", t=11)

        def s3(ap):
            return ap.rearrange("p (s t) -> p s t", t=8)

        hh_store = {}

        def A_xa(n):
            def c_xa(j, bk, bB, n=n):
                c = 2 * n + j
                if nP:
                    evac_bias(xe[:, j, 3:3 + TPc], bk[:, 0:TPc], V_BIN + c, bB, [xeB[j]], "act")
                    S.op("dve", lambda en: en.tensor_copy(out=xe[:, j, 0:3], in_=halo[:, c, :]),
                         reads=[haloB], writes=[xeB[j]])
                if hasS:
                    evac_bias(xes(j)[:, :, 3:11], s3(bk[:, so:so + 128]), V_BIN + c, bB, [xeB[j]], "act")
                    S.op("dve", lambda en: en.tensor_copy(out=xes(j)[:, :, 0:3], in_=cin[:, c]),
                         reads=[aB[12], aB[13]], writes=[xeB[j]])
            fm_proj(W["w_in"], C_XA + n * 256, u_rhs(T), uB, 16, T, c_xa)

        def A_conv(n):
            for j in range(2):
                c = 2 * n + j
                parts = []
                if nP:
                    parts.append((arena[:, j, 0:TPc], lambda k, j=j: xe[:, j, k:k + TPc]))
                if hasS:
                    parts.append((s3(arena[:, j, so:so + 128]), lambda k, j=j: xes(j)[:, :, k:k + 8]))
                for xc_, sh in parts:
                    S.op("dve", lambda en: en.tensor_scalar(out=xc_, in0=sh(0), scalar1=vcol(V_CW + c), scalar2=vcol(V_CB + c),
                                                             op0=ALU.mult, op1=ALU.add), reads=[xeB[j], vB], writes=[aB[j]])
                    for k in range(1, 4):
                        S.op("dve", lambda en, k=k: en.scalar_tensor_tensor(
                            out=xc_, in0=sh(k), scalar=vcol(V_CW + 16 * k + c), in1=xc_, op0=ALU.mult, op1=ALU.add),
                            reads=[xeB[j], vB, aB[j]], writes=[aB[j]])
                S.op("act", lambda en, j=j: en.copy(out=arena_b[:, 2, j * 512:j * 512 + T], in_=arena[:, j, 0:T]),
                     reads=[aB[j]], writes=[aB[2]])
                if nP:
                    S.op("dve", lambda en, j=j, c=c: en.tensor_copy(out=halo[:, c, :], in_=xe[:, j, TPc:TPc + 3]),
                         reads=[xeB[j]], writes=[haloB])
            tails = []
            if hasS:
                tails.append((48, True))
            if last:
                tails.append((3, False))
            for R, is_s in tails:
                for j in range(2):
                    ctmp = arena[:, 10, j * 64:j * 64 + R]
                    if is_s:
                        S.op("dve", lambda en, j=j, ctmp=ctmp: en.tensor_copy(
                            out=ctmp.rearrange("p (s r) -> p s r", r=3), in_=xes(j)[:, :, 8:11]),
                            reads=[xeB[j]], writes=[aB[10]])
                    else:
                        S.op("dve", lambda en, j=j, ctmp=ctmp: en.tensor_copy(out=ctmp, in_=xe[:, j, TPc:TPc + 3]),
                             reads=[xeB[j]], writes=[aB[10]])
                bk, bB = nbank()
                transposes([bk[0:R, j * 128:(j + 1) * 128] for j in range(2)],
                           [arena[:, 10, j * 64:j * 64 + R] for j in range(2)], identf, idB, [aB[10]], bB)
                S.op("act", lambda en, bk=bk, R=R: en.copy(out=arena[0:R, 11, 0:256], in_=bk[0:R, 0:256]),
                     reads=[bB], writes=[aB[11]])
                S.dma("sp", (convs if is_s else convp)[:, n * 256:(n + 1) * 256], arena[0:R, 11, 0:256],
                      reads=[aB[11]], is_out=True)
        def A_gates(n):
            for wi, (wname, bcolbase, aslot) in enumerate((("lru_wa", V_BA, 3), ("lru_wx", V_BX, 4))):
                def cgate(j, bk, bB, aslot=aslot, bcolbase=bcolbase, n=n):
                    c = 2 * n + j
                    dst = arena[:, aslot + 2 * j, 0:T]
                    S.op("act", lambda en: en.activation(out=dst, in_=bk[:, 0:T], func=AF.Sigmoid,
                                                          bias=vcol(bcolbase + c), scale=1.0),
                         reads=[bB, vB], writes=[aB[aslot + 2 * j]])
                fm_proj(W[wname], 0, lambda kc: arena_b[:, 2, kc * 512:kc * 512 + T], [aB[2], aB[2]], 2, T, cgate, r0=n * 256)
        def A_chain(n):
            hhs = {}
            hh_store[n] = hhs
            A_ = [arena[:, 3 + 2 * j, 0:T] for j in range(2)]; ABs = [aB[3 + 2 * j] for j in range(2)]
            G_ = [arena[:, 4 + 2 * j, 0:T] for j in range(2)]; GBs = [aB[4 + 2 * j] for j in range(2)]
            M_ = [arena[:, 7, 0:T], arena[:, 2, 0:T]]; MBs = [aB[7], aB[2]]
            for j in range(2):
                S.op("dve", lambda en, j=j: en.tensor_tensor(out=G_[j], in0=G_[j], in1=arena[:, j, 0:T], op=ALU.mult),
                     reads=[GBs[j], aB[j]], writes=[GBs[j]])
            for j in range(2):
                c = 2 * n + j
                S.op("act", lambda en, j=j, c=c: en.activation(out=A_[j], in_=A_[j], func=AF.Exp, scale=cl[:, c:c + 1]),
                     reads=[ABs[j], clB], writes=[ABs[j]])
            for j in range(2):
                S.op("act", lambda en, j=j: en.activation(out=M_[j], in_=A_[j], func=AF.Square),
                     reads=[ABs[j]], writes=[MBs[j]])
            for j in range(2):
                S.op("act", lambda en, j=j: en.activation(out=M_[j], in_=M_[j], func=AF.Ln, scale=-1.0, bias=1.0),
                     reads=[MBs[j]], writes=[MBs[j]])
            for j in range(2):
                S.op("act", lambda en, j=j: en.activation(out=M_[j], in_=M_[j], func=AF.Exp, scale=0.5),
                     reads=[MBs[j]], writes=[MBs[j]])
            for j in range(2):
                c = 2 * n + j
                a_ = A_[j]; aBj = ABs[j]; gi = G_[j]; giB = GBs[j]; m_ = M_[j]; mB = MBs[j]
                if first_mode == "always":
                    S.op("dve", lambda en: en.memset(m_[:, 0:1], 1.0), reads=[], writes=[mB])
                elif first_mode == "flagA":
                    S.op("dve", lambda en: en.tensor_scalar(out=m_[:, 0:1], in0=m_[:, 0:1], scalar1=flag[:, 0:1],
                                                             scalar2=flag[:, 1:2], op0=ALU.mult, op1=ALU.add),
                         reads=[mB, flB], writes=[mB])
                S.op("dve", lambda en: en.tensor_tensor(out=gi, in0=gi, in1=m_, op=ALU.mult), reads=[giB, mB], writes=[giB])
                hh = arena[:, 8, 0:T]; hhB = aB[8]
                if hasS:
                    a3 = s3(a_[:, so:so + 128])
                    g3 = s3(gi[:, so:so + 128])
                    tmp = arena[:, 10, 256:272]
                    S.op("dve", lambda en: en.tensor_tensor(out=tmp, in0=a3[:, :, 0], in1=h0fm[:, c, :], op=ALU.mult),
                         reads=[aBj, h0B], writes=[aB[10]])
                    S.op("dve", lambda en: en.tensor_tensor(out=g3[:, :, 0], in0=g3[:, :, 0], in1=tmp, op=ALU.add),
                         reads=[giB, aB[10]], writes=[giB])
                    S.op("dve", lambda en: en.memset(a3[:, :, 0], 0.0), reads=[], writes=[aBj])
                if nP:
                    S.op("dve", lambda en: en.tensor_tensor_scan(out=hh, data0=a_, data1=gi, initial=hprev[:, c:c + 1],
                                                                  op0=ALU.mult, op1=ALU.add),
                         reads=[aBj, giB, hpB], writes=[hhB])
                    S.op("dve", lambda en: en.tensor_copy(out=hprev[:, c:c + 1], in_=hh[:, TPc - 1:TPc]),
                         reads=[hhB], writes=[hpB])
                else:
                    S.op("dve", lambda en: en.tensor_tensor_scan(out=hh, data0=a_, data1=gi, initial=0.0,
                                                                  op0=ALU.mult, op1=ALU.add),
                         reads=[aBj, giB], writes=[hhB])
                if hasS:
                    S.op("dve", lambda en: en.tensor_copy(out=arena[:, 10, 320 + j * 16:320 + (j + 1) * 16],
                                                          in_=s3(hh[:, so:so + 128])[:, :, 7]),
                         reads=[hhB], writes=[aB[10]])
                hhs[j] = (hh, hhB)
                if j == 0 and not state_only:
                    keep = arena[:, 9, 0:T]
                    S.op("act", lambda en, keep=keep, hh=hh: en.copy(out=keep, in_=hh), reads=[hhB], writes=[aB[9]])
                    hhs[0] = (keep, aB[9])
            if hasS:
                bk, bB = nbank()
                transposes([bk[0:16, j * 128:(j + 1) * 128] for j in range(2)],
                           [arena[:, 10, 320 + j * 16:320 + (j + 1) * 16] for j in range(2)], identf, idB, [aB[10]], bB)
                S.op("act", lambda en, bk=bk: en.copy(out=arena[0:16, 11, 256:512], in_=bk[0:16, 0:256]),
                     reads=[bB], writes=[aB[11]])
                S.dma("sp", lrus[:, n * 256:(n + 1) * 256], arena[0:16, 11, 256:512], reads=[aB[11]], is_out=True)

        def A_ga(n):
            hhs = hh_store[n]

            def c_ga(j, bk, bB, n=n, hhs=hhs):
                c = 2 * n + j
                hh, hhB = hhs[j]
                xg = arena[:, 3, 0:T]; t_ = arena[:, 4, 0:T]
                evac_bias(xg, bk[:, 0:T], V_BIN + 16 + c, bB, [aB[3]], "act")
                S.op("dve", lambda en: en.tensor_tensor(out=t_, in0=xg, in1=xg, op=ALU.mult), reads=[aB[3]], writes=[aB[4]])
                S.op("dve", lambda en: en.tensor_scalar(out=t_, in0=t_, scalar1=0.044715, scalar2=1.0,
                                                         op0=ALU.mult, op1=ALU.add), reads=[aB[4]], writes=[aB[4]])
                S.op("dve", lambda en: en.tensor_tensor(out=t_, in0=t_, in1=xg, op=ALU.mult), reads=[aB[4], aB[3]], writes=[aB[4]])
                S.op("act", lambda en: en.activation(out=t_, in_=t_, func=AF.Sigmoid, scale=1.5957691216057308),
                     reads=[aB[4]], writes=[aB[4]])
                S.op("dve", lambda en: en.tensor_tensor(out=t_, in0=t_, in1=xg, op=ALU.mult), reads=[aB[4], aB[3]], writes=[aB[4]])
                S.op("dve", lambda en: en.tensor_tensor(out=scr[:, c, 0:T], in0=t_, in1=hh, op=ALU.mult),
                     reads=[aB[4], hhB], writes=[sB[c]])
            fm_proj(W["w_in"], C_GA + n * 256, u_rhs(T), uB, 16, T, c_ga)

        A_xa(0)
        A_conv(0)
        for n in range(8):
            if n + 1 < 8:
                A_xa(n + 1)
            A_gates(n)
            A_chain(n)
            if n + 1 < 8:
                A_conv(n + 1)
            if not state_only:
                A_ga(n)
        if last:
            bk, bB = nbank()
            transposes([bk[0:16, 0:128]], [hprev[:, 0:16]], identf, idB, [hpB], bB)
            S.op("act", lambda en: en.copy(out=arena[0:16, 11, 256:384], in_=bk[0:16, 0:128]), reads=[bB], writes=[aB[11]])
            S.dma("sp", lrup[:, :], arena[0:16, 11, 256:384], reads=[aB[11]], is_out=True)

        for og in range(0 if state_only else 8):
            pb_ = {}

            def c_pa(j, bk, bB):
                pb_[j] = (bk, bB)
            fm_proj(W["proj_a"], og * 256, lambda kc: scr[:, kc, 0:T], sB, 16, T, c_pa)

            def c_gta(j, bk, bB, og=og):
                oc = og * 2 + j
                sl_ = 10 + (oc % 2)
                sg = arena[:, sl_, 0:T]
                S.op("act", lambda en: en.activation(out=sg, in_=bk[:, 0:T], func=AF.Sigmoid,
                                                      bias=vcol(V_BIN + C_GTA // 128 + oc), scale=1.0),
                     reads=[bB, vB], writes=[aB[sl_]])
                S.op("dve", lambda en: en.tensor_tensor(out=scr[:, 32 + oc, 0:T], in0=sg, in1=pb_[j][0][:, 0:T], op=ALU.mult),
                     reads=[aB[sl_], pb_[j][1]], writes=[sB[32 + oc]])
            fm_proj(W["w_in"], C_GTA + og * 256, u_rhs(T), uB, 16, T, c_gta)

        for (c0_, n_, cd_, sd_) in cs:
            S.dma("sp", cos_sb[:, c0_:c0_ + n_], cd_, writes=[csB])
            S.dma("sp", sin_sb[:, c0_:c0_ + n_], sd_, writes=[csB])
        for h in range(NH):
            gam = 1.0 - 2.0 ** (-5.0 - h)
            if not state_only:
                if nP:
                    S.dma("sp", mk_sb[:, 0, :], mkp_d[h * 128:(h + 1) * 128, :], writes=[mkB[0]])
                    S.dma("sp", cd_sb[:, 0, 0:TPc], cdp_d[h * 128:(h + 1) * 128, 0:TPc], writes=[cdB[0]])
                if hasS:
                    S.dma("sp", mk_sb[:, 1, :], mks_d[h * 128:(h + 1) * 128, :], writes=[mkB[1]])
                    S.dma("sp", cd_sb[:, 0, so:so + 128], cds_d[h * 128:(h + 1) * 128, :], writes=[cdB[0]])
            S.dma("sp", bvf[:, :], bv_d[:, h * 512:(h + 1) * 512], writes=[bvfB])
            S.op("dve", lambda en: en.tensor_copy(out=bvb[:, :], in_=bvf[:, :]), reads=[bvfB], writes=[bvB])
            qf = arena[:, 0:2, :]; kf = arena[:, 2:4, :]

            def c_q(j, bk, bB, h=h):
                S.op("dve", lambda en: en.scalar_tensor_tensor(
                    out=qf[:, j, 0:T], in0=bk[:, 0:T], scalar=vcol(V_BIN + C_Q // 128 + 2 * h + j), in1=cd_sb[:, 0, 0:T],
                    op0=ALU.add, op1=ALU.mult), reads=[bB, vB, cdB[0]], writes=[aB[j]])
            if not state_only:
                fm_proj(W["w_in"], C_Q + h * 256, u_rhs(T), uB, 16, T, c_q)

            def c_k(j, bk, bB, h=h):
                evac_bias(kf[:, j, 0:T], bk[:, 0:T], V_BIN + C_K // 128 + 2 * h + j, bB, [aB[2 + j]], "act")
            fm_proj(W["w_in"], C_K + h * 256, u_rhs(T), uB, 16, T, c_k)
            vtm = lambda tc: arena_b[:, 10 + tc // 2, (tc % 2) * 512:(tc % 2) * 512 + 512]
            vtmB = lambda tc: aB[10 + tc // 2]
            vb = [nbank() for _ in range(ntc)]
            for s_ in range(2):
                c0 = C_V + h * 512 + s_ * 256
                sl, slB = slab(W["w_in"], 0, 16, c0)
                for tc in range(ntc):
                    bk, bB = vb[tc]
                    pairs = [(u_bf[:, kc, tc * 128:(tc + 1) * 128], sl[:, kc, 0:256]) for kc in range(16)]
                    pairs.append((onesb[0:1, 0:128], bvb[0:1, s_ * 256:(s_ + 1) * 256]))
                    mm_group(bk[:, s_ * 256:(s_ + 1) * 256], bB, pairs, [slB, onB, bvB] + uB)
            for tc in range(ntc):
                S.op("act", lambda en, tc=tc: en.copy(out=vtm(tc), in_=vb[tc][0][:, :]), reads=[vb[tc][1]], writes=[vtmB(tc)])
            for (src, sBs, dslot) in (((kf, [aB[2], aB[3]], 8),) if state_only else ((qf, [aB[0], aB[1]], 6), (kf, [aB[2], aB[3]], 8))):
                t1 = arena[:, 4, 0:T]; t2 = arena[:, 5, 0:T]
                x1 = src[:, 0, 0:T]; x2 = src[:, 1, 0:T]
                d1 = arena_b[:, dslot, 0:T]; d2 = arena_b[:, dslot, 512:512 + T]
                S.op("dve", lambda en, x1=x1: en.tensor_tensor(out=t1, in0=x1, in1=cos_sb[:, 0:T], op=ALU.mult),
                     reads=[sBs[0], csB], writes=[aB[4]])
                S.op("dve", lambda en, x2=x2: en.tensor_tensor(out=t2, in0=x2, in1=sin_sb[:, 0:T], op=ALU.mult),
                     reads=[sBs[1], csB], writes=[aB[5]])
                S.op("dve", lambda en, d1=d1: en.tensor_tensor(out=d1, in0=t1, in1=t2, op=ALU.subtract),
                     reads=[aB[4], aB[5]], writes=[aB[dslot]])
                S.op("dve", lambda en, x2=x2: en.tensor_tensor(out=t1, in0=x2, in1=cos_sb[:, 0:T], op=ALU.mult),
                     reads=[sBs[1], csB], writes=[aB[4]])
                S.op("dve", lambda en, x1=x1: en.tensor_tensor(out=t2, in0=x1, in1=sin_sb[:, 0:T], op=ALU.mult),
                     reads=[sBs[0], csB], writes=[aB[5]])
                S.op("dve", lambda en, d2=d2: en.tensor_tensor(out=d2, in0=t1, in1=t2, op=ALU.add),
                     reads=[aB[4], aB[5]], writes=[aB[dslot]])
            qdT = lambda kc, lo, n_: arena_b[:, 6, kc * 512 + lo:kc * 512 + lo + n_]
            kT = lambda kc, lo, n_: arena_b[:, 8, kc * 512 + lo:kc * 512 + lo + n_]
            ktm = arena_b[:, 9, :].rearrange("p (t f) -> p t f", f=256)
            for tc in range(ntc):
                sc_ = h if tc < nP else 8 + h
                transposes([bank7[:, kc * 128:(kc + 1) * 128] for kc in range(2)],
                           [kT(kc, tc * 128, 128) for kc in range(2)], identb, idbB, [aB[8]], b7B)
                S.op("dve", lambda en, tc=tc, sc_=sc_: en.tensor_scalar(
                    out=ktm[:, tc, :], in0=bank7[:, 0:256], scalar1=sdec[:, sc_:sc_ + 1],
                    scalar2=None, op0=ALU.mult), reads=[b7B, sdB], writes=[aB[9]])
            o_sb = arena[:, 0:4, :]
            if nP:
                Sc, ScB = nsbuf()
                if s_mode == "zero":
                    S.op("dve", lambda en: en.memset(Sc.rearrange("p k v -> p (k v)"), 0.0), writes=[ScB])
                else:
                    S.dma("sp", Sc, sscr[h * 256:(h + 1) * 256, :].rearrange("(k p) v -> p k v", p=128),
                          reads=[sscrB[h]], writes=[ScB])
                    if s_mode == "flag":
                        S.op("dve", lambda en: en.tensor_scalar(out=Sc.rearrange("p k v -> p (k v)"),
                                                                 in0=Sc.rearrange("p k v -> p (k v)"),
                                                                 scalar1=flag[:, 0:1], scalar2=None, op0=ALU.mult),
                             reads=[flB, ScB], writes=[ScB])
            def store_state(h=h):
                S.dma("sp", sscr[h * 256:(h + 1) * 256, :].rearrange("(k p) v -> p k v", p=128), Sc,
                      reads=[ScB], writes=[sscrB[h]])
                if last:
                    S.dma("sp", retp[h * 256:(h + 1) * 256, :].rearrange("(k p) v -> p k v", p=128), Sc,
                          reads=[ScB], is_out=True)

            for tc in range(ntc):
                is_s = tc >= nP
                if state_only:
                    cdec = float(gam ** 128)
                    for kc in range(2):
                        bk, bB = nbank()
                        mm_group(bk[:, :], bB, [(ktm[:, tc, kc * 128:(kc + 1) * 128], vtm(tc))], [aB[9], vtmB(tc)])
                        S.op("dve", lambda en, kc=kc, bk=bk: en.scalar_tensor_tensor(
                            out=Sc[:, kc, :], in0=Sc[:, kc, :], scalar=cdec, in1=bk[:, :],
                            op0=ALU.mult, op1=ALU.add), reads=[bB, ScB], writes=[ScB])
                    if tc == nP - 1:
                        store_state()
                    continue
                bk, bB = nbank()
                mm_group(bk[:, 0:128], bB, [(kT(kc, tc * 128, 128), qdT(kc, tc * 128, 128)) for kc in range(2)],
                         [aB[8], aB[6]])
                PT = arena_b[:, 13, 0:128]
                mi = 1 if is_s else 0
                S.op("dve", lambda en, bk=bk, mi=mi: en.tensor_tensor(out=PT, in0=bk[:, 0:128], in1=mk_sb[:, mi, :], op=ALU.mult),
                     reads=[bB, mkB[mi]], writes=[aB[13]])
                ob_, obB = nbank()
                if not is_s:
                    Sb = arena_b[:, 12, :].rearrange("p (k v) -> p k v", k=2)
                    S.op("act", lambda en: en.copy(out=Sb, in_=Sc), reads=[ScB], writes=[aB[12]])
                    for vc in range(4):
                        pairs = [(vtm(tc)[:, vc * 128:(vc + 1) * 128], PT)]
                        pairs += [(Sb[:, kc, vc * 128:(vc + 1) * 128], qdT(kc, tc * 128, 128)) for kc in range(2)]
                        mm_group(ob_[:, vc * 128:(vc + 1) * 128], obB, pairs, [vtmB(tc), aB[13], aB[12], aB[6]])
                else:
                    for vc in range(4):
                        mm_group(ob_[:, vc * 128:(vc + 1) * 128], obB, [(vtm(tc)[:, vc * 128:(vc + 1) * 128], PT)],
                                 [vtmB(tc), aB[13]], start=(vc == 0), stop=False)
                    cdec_s = float(gam ** 8)
                    PF = 3
                    pend = []

                    def s0_load(sq, h=h):
                        rr = (sq * NH + h) * 256
                        buf, bufB = nsbuf()
                        S.dma("sp", buf, sret[rr:rr + 256, :].rearrange("(k p) v -> p k v", p=128), writes=[bufB])
                        pend.append((buf, bufB))
                    for sq in range(min(PF, NS)):
                        s0_load(sq)
                    for s_ in range(NS):
                        r0 = (s_ * NH + h) * 256
                        S0f, S0fB = pend.pop(0)
                        S0b = arena_b[:, 7, :].rearrange("p (k v) -> p k v", k=2)
                        S.op("act", lambda en, S0f=S0f: en.copy(out=S0b, in_=S0f), reads=[S0fB], writes=[aB[7]])
                        for vc in range(4):
                            mm_group(ob_[:, vc * 128 + s_ * 8:vc * 128 + s_ * 8 + 8], obB,
                                     [(S0b[:, kc, vc * 128:(vc + 1) * 128], qdT(kc, tc * 128 + s_ * 8, 8)) for kc in range(2)],
                                     [aB[7], aB[6]], start=False, stop=(s_ == NS - 1))
                        km = arena_b[:, 12, 0:256]
                        S.op("dve", lambda en, s_=s_, tc=tc: en.tensor_scalar(out=km, in0=ktm[:, tc, :], scalar1=seqm[:, s_:s_ + 1],
                                                                               scalar2=None, op0=ALU.mult),
                             reads=[aB[9], sqB], writes=[aB[12]])
                        for kc in range(2):
                            bk, bB = nbank(avoid=ob_)
                            mm_group(bk[:, :], bB, [(km[:, kc * 128:(kc + 1) * 128], vtm(tc))], [aB[12], vtmB(tc)])
                            S.op("dve", lambda en, kc=kc, bk=bk, S0f=S0f: en.scalar_tensor_tensor(
                                out=S0f[:, kc, :], in0=S0f[:, kc, :], scalar=cdec_s, in1=bk[:, :],
                                op0=ALU.mult, op1=ALU.add), reads=[bB, S0fB], writes=[S0fB])
                        S.dma("sp", rets[r0:r0 + 256, :].rearrange("(k p) v -> p k v", p=128), S0f, reads=[S0fB], is_out=True)
                        if s_ + PF < NS:
                            s0_load(s_ + PF)
                e = alt()
                dst = o_sb[:, :, tc * 128:(tc + 1) * 128]
                srcv = ob_[:, :].rearrange("p (v t) -> p v t", v=4)
                if e == "act":
                    S.op("act", lambda en: en.copy(out=dst, in_=srcv), reads=[obB], writes=aB[0:4])
                else:
                    S.op("dve", lambda en: en.tensor_copy(out=dst, in_=srcv), reads=[obB], writes=aB[0:4])
                if not is_s:
                    cdec = float(gam ** 128)
                    for kc in range(2):
                        bk, bB = nbank()
                        mm_group(bk[:, :], bB, [(ktm[:, tc, kc * 128:(kc + 1) * 128], vtm(tc))], [aB[9], vtmB(tc)])
                        S.op("dve", lambda en, kc=kc, bk=bk: en.scalar_tensor_tensor(
                            out=Sc[:, kc, :], in0=Sc[:, kc, :], scalar=cdec, in1=bk[:, :],
                            op0=ALU.mult, op1=ALU.add), reads=[bB, ScB], writes=[ScB])
                    if tc == nP - 1:
                        store_state()
            if state_only:
                continue
            rs, rsB = rstd_of(lambda vc: o_sb[:, vc, 0:T], 4, aB[0:4], T, 1.0 / 512.0, sq_slot=12, rs_slot=5)
            for vc in range(4):
                S.op("dve", lambda en, vc=vc: en.scalar_tensor_tensor(
                    out=o_sb[:, vc, 0:T], in0=o_sb[:, vc, 0:T], scalar=vcol(V_RN + h * 4 + vc), in1=rs,
                    op0=ALU.mult, op1=ALU.mult), reads=[aB[vc], rsB, vB], writes=[aB[vc]])
            for s_ in range(2):
                def c_gr(j, bk, bB, s_=s_, h=h):
                    vc = s_ * 2 + j
                    sg = arena[:, 4, 0:T]
                    S.op("act", lambda en: en.activation(out=sg, in_=bk[:, 0:T], func=AF.Silu,
                                                          bias=vcol(V_BIN + C_GR // 128 + h * 4 + vc), scale=1.0),
                         reads=[bB, vB], writes=[aB[4]])
                    S.op("dve", lambda en: en.tensor_tensor(out=scr[:, h * 4 + vc, 0:T], in0=sg, in1=o_sb[:, vc, 0:T], op=ALU.mult),
                         reads=[aB[4], aB[vc]], writes=[sB[h * 4 + vc]])
                fm_proj(W["w_in"], C_GR + h * 512 + s_ * 256, u_rhs(T), uB, 16, T, c_gr)
        for og in range(0 if state_only else 8):
            pb_ = {}

            def c_pb(j, bk, bB):
                pb_[j] = (bk, bB)
            fm_proj(W["proj_b"], og * 256, lambda kc: scr[:, kc, 0:T], sB, 32, T, c_pb)

            def c_gtb(j, bk, bB, og=og):
                oc = og * 2 + j
                sl_ = 10 + (oc % 2)
                sg = arena[:, sl_, 0:T]
                S.op("act", lambda en: en.activation(out=sg, in_=bk[:, 0:T], func=AF.Sigmoid,
                                                      bias=vcol(V_BIN + C_GTB // 128 + oc), scale=1.0),
                     reads=[bB, vB], writes=[aB[sl_]])
                S.op("dve", lambda en: en.tensor_tensor(out=sg, in0=sg, in1=pb_[j][0][:, 0:T], op=ALU.mult),
                     reads=[aB[sl_], pb_[j][1]], writes=[aB[sl_]])
                S.op("dve", lambda en: en.tensor_tensor(out=scr[:, 32 + oc, 0:T], in0=sg, in1=scr[:, 32 + oc, 0:T], op=ALU.add),
                     reads=[aB[sl_], sB[32 + oc]], writes=[sB[32 + oc]])
            fm_proj(W["w_in"], C_GTB + og * 256, u_rhs(T), uB, 16, T, c_gtb)
        for og in range(0 if state_only else 8):
            def c_wo(j, bk, bB, og=og):
                oc = og * 2 + j
                S.op("dve", lambda en: en.tensor_tensor(out=x_fm[:, oc, 0:T], in0=x_fm[:, oc, 0:T], in1=bk[:, 0:T], op=ALU.add),
                     reads=[bB, xB[oc]], writes=[xB[oc]])
            fm_proj(W["w_out"], og * 256, lambda kc: scr[:, 32 + kc, 0:T], sB[32:48], 16, T, c_wo)

    def ple_and_out(psrcs, ydsts):
        T = 128 * len(psrcs)
        rmsnorm_u(V_PLE, T)
        for tc, psrc in enumerate(psrcs):
            pin = arena[:, 8, 0:256]
            S.dma("sp", pin, psrc, writes=[aB[8]])
            bk, bB = nbank()
            transposes([bk[:, j * 128:(j + 1) * 128] for j in range(2)],
                       [arena[:, 8, j * 128:(j + 1) * 128] for j in range(2)], identf, idB, [aB[8]], bB)
            S.op("act", lambda en, tc=tc, bk=bk: en.copy(
                out=arena_b[:, 9, :].rearrange("p (k t) -> p k t", k=2)[:, :, tc * 128:(tc + 1) * 128],
                in_=bk[:, 0:256].rearrange("p (k t) -> p k t", k=2)), reads=[bB], writes=[aB[9]])
        for og in range(8):
            pb_ = {}

            def c_pp(j, bk, bB):
                pb_[j] = (bk, bB)
            fm_proj(W["ple_proj"], og * 256, lambda kc: arena_b[:, 9, kc * 512:kc * 512 + T], [aB[9], aB[9]], 2, T, c_pp)

            def c_pg(j, bk, bB, og=og):
                oc = og * 2 + j
                sl_ = 10 + (oc % 2)
                sg = arena[:, sl_, 0:T]
                S.op("act", lambda en: en.activation(out=sg, in_=bk[:, 0:T], func=AF.Sigmoid, bias=vcol(V_PBG + oc), scale=1.0),
                     reads=[bB, vB], writes=[aB[sl_]])
                S.op("dve", lambda en: en.tensor_tensor(out=sg, in0=sg, in1=pb_[j][0][:, 0:T], op=ALU.mult),
                     reads=[aB[sl_], pb_[j][1]], writes=[aB[sl_]])
                S.op("dve", lambda en: en.tensor_tensor(out=x_fm[:, oc, 0:T], in0=x_fm[:, oc, 0:T], in1=sg, op=ALU.add),
                     reads=[aB[sl_], xB[oc]], writes=[xB[oc]])
            fm_proj(W["ple_wg"], og * 256, u_rhs(T), uB, 16, T, c_pg)
        rs, rsB = rstd_of(lambda kc: x_fm[:, kc, 0:T], 16, xB, T, 1.0 / DM)
        for kc in range(16):
            S.op("dve", lambda en, kc=kc: en.scalar_tensor_tensor(
                out=x_fm[:, kc, 0:T], in0=x_fm[:, kc, 0:T], scalar=vcol(V_FIN + kc), in1=rs,
                op0=ALU.mult, op1=ALU.mult), reads=[xB[kc], rsB, vB], writes=[xB[kc]])
        for tc, ydst in enumerate(ydsts):
            a0 = 4 * (tc % 2)
            for g in range(4):
                bk, bB = nbank()
                transposes([bk[:, j * 128:(j + 1) * 128] for j in range(4)],
                           [x_fm[:, 4 * g + j, tc * 128:(tc + 1) * 128] for j in range(4)], identf, idB,
                           xB[4 * g:4 * g + 4], bB)
                e = alt()
                if e == "act":
                    S.op("act", lambda en, g=g, bk=bk: en.copy(out=arena[:, a0 + g, :], in_=bk[:, :]), reads=[bB], writes=[aB[a0 + g]])
                else:
                    S.op("dve", lambda en, g=g, bk=bk: en.tensor_copy(out=arena[:, a0 + g, :], in_=bk[:, :]), reads=[bB], writes=[aB[a0 + g]])
            S.dma("sp", ydst.rearrange("t (a b) -> t a b", a=4),
                  arena[:, a0:a0 + 4, :], reads=aB[a0:a0 + 4], is_out=True)

    def rows(d, r0, n=128):
        return d[r0:r0 + n, :]

    for ti in range(2):
        load_x([rows(xq, ti * 512 + k * 128) for k in range(4)])
        ffn("ffn1_wg", "ffn1_wu", "ffn1_wd", V_FFN1, 512)
        mixer(4, False, state_only=True, first_mode=("always" if ti == 0 else None), s_mode=("zero" if ti == 0 else None),
              cs=[(0, 512, cosq_d[:, ti * 512:(ti + 1) * 512], sinq_d[:, ti * 512:(ti + 1) * 512])])
    S.op("dve", lambda en: en.tensor_scalar(out=hprev[:, :], in0=hprev[:, :], scalar1=flag[:, 0:1], scalar2=None, op0=ALU.mult),
         reads=[flB, hpB], writes=[hpB])
    S.op("dve", lambda en: en.tensor_scalar(out=halo[:].rearrange("p c r -> p (c r)"), in0=halo[:].rearrange("p c r -> p (c r)"),
                                             scalar1=flag[:, 0:1], scalar2=None, op0=ALU.mult),
         reads=[flB, haloB], writes=[haloB])
    for ti in range(3):
        nP = 3 if ti < 2 else 2
        hasS = (ti == 2)
        T = 384
        r0 = ti * 384
        xsrc = [rows(xp, r0 + k * 128) for k in range(nP)] + ([rows(xs, 0)] if hasS else [])
        psrc = [rows(pp, r0 + k * 128) for k in range(nP)] + ([rows(ps, 0)] if hasS else [])
        ydst = [rows(yp, r0 + k * 128) for k in range(nP)] + ([rows(ys, 0)] if hasS else [])
        cs = [(0, nP * 128, cosp_d[:, r0:r0 + nP * 128], sinp_d[:, r0:r0 + nP * 128])]
        if hasS:
            cs.append((nP * 128, 128, coss_d[:, :], sins_d[:, :]))
        load_x(xsrc)
        ffn("ffn1_wg", "ffn1_wu", "ffn1_wd", V_FFN1, T)
        mixer(nP, hasS, first_mode=("flagA" if ti == 0 else None), last=(ti == 2), cs=cs,
              s_mode=("flag" if ti == 0 else None))
        ffn("ffn2_wg", "ffn2_wu", "ffn2_wd", V_FFN2, T)
        ple_and_out(psrc, ydst)
    S.finish()
    return nc, es


def _consts():
    f32 = np.float32
    h = np.arange(NH)
    gam = 1.0 - np.exp2(-5.0 - h)
    inv = (10000.0 ** (-(np.arange(128, dtype=f32)) / f32(128))).astype(f32)
    posp = np.arange(2048, dtype=f32)
    angp = (posp[None, :] * inv[:, None]).astype(f32)
    poss = (16384 + (np.arange(128) % 8)).astype(f32)
    angs = (poss[None, :] * inv[:, None]).astype(f32)
    c = {}
    c["identf"] = np.eye(128, dtype=f32)
    c["cos_all"] = np.cos(angp).astype(f32); c["sin_all"] = np.sin(angp).astype(f32)
    c["coss"] = np.cos(angs).astype(f32); c["sins"] = np.sin(angs).astype(f32)
    j = np.arange(128)[:, None]; i = np.arange(128)[None, :]
    mkp = np.zeros((NH, 128, 128)); mks = np.zeros((NH, 128, 128))
    cdp = np.zeros((NH, 128, 512)); cds = np.zeros((NH, 128, 128))
    sdec = np.zeros((128, 16))
    for hh in range(NH):
        g = gam[hh]
        mkp[hh] = np.where(i >= j, g ** (-(j + 1.0)), 0.0) / 16.0
        tj = j % 8; ti = i % 8
        mks[hh] = np.where((j // 8 == i // 8) & (ti >= tj), g ** (-(tj + 1.0)), 0.0) / 16.0
        cdp[hh] = (g ** ((np.arange(512) % 128) + 1.0))[None, :]
        cds[hh] = (g ** ((np.arange(128) % 8) + 1.0))[None, :]
        sdec[:, hh] = g ** (127.0 - np.arange(128)) / 16.0
        sdec[:, 8 + hh] = g ** (7.0 - (np.arange(128) % 8)) / 16.0
    c["mkp"] = mkp.reshape(NH * 128, 128).astype(f32); c["mks"] = mks.reshape(NH * 128, 128).astype(f32)
    c["cdp"] = cdp.reshape(NH * 128, 512).astype(f32); c["cds"] = cds.reshape(NH * 128, 128).astype(f32)
    c["sdec"] = sdec.astype(f32)
    c["seqm"] = (np.arange(128)[:, None] // 8 == np.arange(16)[None, :]).astype(f32)
    return c


def _fm(v):
    return np.ascontiguousarray(np.asarray(v, dtype=np.float32).reshape(-1, 128).T)


_CACHE = {}


def kernel(**inp):
    f32 = np.float32
    g = {k: np.asarray(v) for k, v in inp.items()}
    vecs = np.concatenate([
        _fm(g["ffn1_norm"][0]), _fm(g["mix_norm"][0]), _fm(g["ffn2_norm"][0]), _fm(g["ple_norm"][0]),
        _fm(g["final_norm"]), _fm(g["b_in"][0]),
        _fm(g["conv_w"][0][0]), _fm(g["conv_w"][0][1]), _fm(g["conv_w"][0][2]), _fm(g["conv_w"][0][3]),
        _fm(g["conv_b"][0]), _fm(g["lru_ba"][0]), _fm(g["lru_bx"][0]), _fm(g["lru_lambda"][0]),
        _fm(g["ret_norm"][0].reshape(-1)), _fm(g["ple_bg"][0])], axis=1).astype(f32)
    assert vecs.shape == (128, NV), vecs.shape
    shared = dict(_consts())
    cos_all = shared.pop("cos_all"); sin_all = shared.pop("sin_all")
    shared["cosq"] = np.ascontiguousarray(cos_all[:, 0:1024]); shared["sinq"] = np.ascontiguousarray(sin_all[:, 0:1024])
    shared["vecs"] = vecs
    shared["bv"] = np.ascontiguousarray(g["b_in"][0][C_V:C_V + 4096].reshape(1, 4096))
    for nm in ("ffn1_wg", "ffn1_wu", "ffn1_wd", "w_in", "proj_a", "proj_b", "w_out",
               "ffn2_wg", "ffn2_wu", "ffn2_wd", "ple_wg", "ple_proj"):
        shared[nm] = g[nm][0]
    shared["lru_wa"] = g["lru_wa"][0].reshape(NH * 256, 256)
    shared["lru_wx"] = g["lru_wx"][0].reshape(NH * 256, 256)
    in_maps = []
    for c in range(NCORES):
        b = c % 4
        hf = c // 4
        m = dict(shared)
        m["xp"] = g["x_prompt"][b, hf * 1024:(hf + 1) * 1024]
        m["xq"] = g["x_prompt"][b, 0:1024]
        m["cosp"] = np.ascontiguousarray(cos_all[:, hf * 1024:(hf + 1) * 1024])
        m["sinp"] = np.ascontiguousarray(sin_all[:, hf * 1024:(hf + 1) * 1024])
        fl = np.zeros((128, 2), np.float32); fl[:, 0] = float(hf); fl[:, 1] = 1.0 - float(hf)
        m["flag"] = fl
        m["xs"] = g["x_sample"][16 * c:16 * c + 16].reshape(TS, DM)
        m["pp"] = g["p_prompt"][0, b, hf * 1024:(hf + 1) * 1024]
        m["ps"] = g["p_sample"][0, 16 * c:16 * c + 16].reshape(TS, 256)
        m["slru"] = g["state_lru"][0, 16 * c:16 * c + 16]
        m["sconv"] = g["state_conv"][0, 16 * c:16 * c + 16].reshape(48, DM)
        m["sret"] = g["state_ret"][0, 16 * c:16 * c + 16].reshape(16 * NH * 256, 512)
        in_maps.append(m)
    if "nc" not in _CACHE:
        _CACHE["nc"] = build_nc()
    nc, _es = _CACHE["nc"]
    res = run_bass_kernel_spmd(nc, in_maps, core_ids=list(range(NCORES)))
    R = res.results
    y_prompt = np.stack([np.concatenate([R[b]["yp"], R[b + 4]["yp"]]) for b in range(4)]).astype(f32)
    y_sample = np.concatenate([R[c]["ys"].reshape(16, 8, DM) for c in range(NCORES)]).astype(f32)
    lru_p = np.stack([R[b + 4]["lrup"].reshape(DM) for b in range(4)])[None].astype(f32)
    conv_p = np.stack([R[b + 4]["convp"] for b in range(4)])[None].astype(f32)
    ret_p = np.stack([R[b + 4]["retp"].reshape(NH, 256, 512) for b in range(4)])[None].astype(f32)
    lru_s = np.concatenate([R[c]["lrus"] for c in range(NCORES)])[None].astype(f32)
    conv_s = np.concatenate([R[c]["convs"].reshape(16, 3, DM) for c in range(NCORES)])[None].astype(f32)
    ret_s = np.concatenate([R[c]["rets"].reshape(16, NH, 256, 512) for c in range(NCORES)])[None].astype(f32)
    return (y_prompt, y_sample, lru_p, conv_p, ret_p, lru_s, conv_s, ret_s)
```

```python
import numpy as np
import concourse.bass as bass
import concourse.mybir as mybir
from concourse.bass_utils import run_bass_kernel_spmd

F32 = mybir.dt.float32
BF16 = mybir.dt.bfloat16
AF = mybir.ActivationFunctionType
ALU = mybir.AluOpType

NCORES = 8
DM = 2048
DFF = 5632
NH = 8
EPS = 1e-6
TP = 512
TS = 128
NPT = 1024 // TP

V_FFN1, V_MIX, V_FFN2, V_PLE, V_FIN, V_BIN = 0, 16, 32, 48, 64, 80
V_CW, V_CB, V_BA, V_BX, V_LAM, V_RN, V_PBG = 240, 304, 320, 336, 352, 368, 400
NV = 416
C_XA, C_GA, C_Q, C_K, C_V, C_GR, C_GTA, C_GTB = 0, 2048, 4096, 6144, 8192, 12288, 16384, 18432

SAME_ENGINE_SYNC = True


class Buf:
    __slots__ = ("name", "lastw", "readers")

    def __init__(self, name):
        self.name = name
        self.lastw = None
        self.readers = {}


class Sched:
    def __init__(self, nc, es):
        self.nc = nc
        self.eng = {"pe": nc.tensor, "act": nc.scalar, "dve": nc.vector, "pool": nc.gpsimd, "sp": nc.sync}
        self.sem = {}
        self.cnt = {}
        self.seen = {k: {} for k in self.eng}
        for k in ("pe", "act", "dve"):
            self.sem[k] = es.enter_context(nc.semaphore("sem_" + k))
            self.cnt[k] = 0
        self.dsems = {"sp": [], "pool": []}
        for q, n in (("sp", 16), ("pool", 8)):
            for i in range(n):
                key = "d_%s_%d" % (q, i)
                self.sem[key] = es.enter_context(nc.semaphore(key))
                self.cnt[key] = 0
                self.dsems[q].append(key)
        self.drr = {"sp": 0, "pool": 0}
        self.out_tags = []

    def _wait(self, e, deps):
        for k, v in deps.items():
            if k == e and (e == "pe" or not SAME_ENGINE_SYNC):
                continue
            if self.seen[e].get(k, 0) < v:
                self.eng[e].wait_ge(self.sem[k], v)
                self.seen[e][k] = v

    @staticmethod
    def _add(deps, tag):
        if tag is not None:
            k, v = tag
            if deps.get(k, 0) < v:
                deps[k] = v

    def _deps(self, reads, writes):
        deps = {}
        for b in reads:
            self._add(deps, b.lastw)
        for b in writes:
            self._add(deps, b.lastw)
            for k, v in b.readers.items():
                self._add(deps, (k, v))
        return deps

    def _commit(self, tag, reads, writes):
        k, v = tag
        for b in reads:
            if b.readers.get(k, 0) < v:
                b.readers[k] = v
        for b in writes:
            b.lastw = tag
            b.readers = {}

    def op(self, e, fn, reads=(), writes=()):
        self._wait(e, self._deps(reads, writes))
        ins = fn(self.eng[e])
        self.cnt[e] += 1
        ins.then_inc(self.sem[e], 1)
        self._commit((e, self.cnt[e]), reads, writes)

    def dma(self, q, out, in_, reads=(), writes=(), is_out=False):
        key = self.dsems[q][self.drr[q] % len(self.dsems[q])]
        self.drr[q] += 1
        deps = self._deps(reads, writes)
        if self.cnt[key] > 0:
            self._add(deps, (key, self.cnt[key]))
        self._wait(q, deps)
        self.cnt[key] += 16
        self.eng[q].dma_start(out=out, in_=in_).then_inc(self.sem[key], 16)
        tag = (key, self.cnt[key])
        self._commit(tag, reads, writes)
        if is_out:
            self.out_tags.append(tag)

    def finish(self):
        deps = {}
        for t in self.out_tags:
            self._add(deps, t)
        self._wait("sp", deps)


def build_nc():
    from contextlib import ExitStack
    nc = bass.Bass("TRN2", target_bir_lowering=False)
    es = ExitStack()

    def DI(name, shape):
        return nc.dram_tensor(name, shape, F32, kind="ExternalInput").ap()

    def DO(name, shape):
        return nc.dram_tensor(name, shape, F32, kind="ExternalOutput").ap()

    xp = DI("xp", [1024, DM]); xq = DI("xq", [1024, DM]); xs = DI("xs", [TS, DM])
    pp = DI("pp", [1024, 256]); ps = DI("ps", [TS, 256])
    slru = DI("slru", [16, DM]); sconv = DI("sconv", [48, DM]); sret = DI("sret", [16 * NH * 256, 512])
    W = {}
    for nm, shp in (("ffn1_wg", [DM, DFF]), ("ffn1_wu", [DM, DFF]), ("ffn1_wd", [DFF, DM]),
                    ("w_in", [DM, 20480]), ("lru_wa", [NH * 256, 256]), ("lru_wx", [NH * 256, 256]),
                    ("proj_a", [DM, DM]), ("proj_b", [4096, DM]), ("w_out", [DM, DM]),
                    ("ffn2_wg", [DM, DFF]), ("ffn2_wu", [DM, DFF]), ("ffn2_wd", [DFF, DM]),
                    ("ple_wg", [DM, DM]), ("ple_proj", [256, DM])):
        W[nm] = DI(nm, shp)
    vecs_d = DI("vecs", [128, NV]); bv_d = DI("bv", [1, 4096])
    identf_d = DI("identf", [128, 128])
    cosp_d = DI("cosp", [128, 1024]); sinp_d = DI("sinp", [128, 1024])
    cosq_d = DI("cosq", [128, 1024]); sinq_d = DI("sinq", [128, 1024])
    flag_d = DI("flag", [128, 2])
    coss_d = DI("coss", [128, 128]); sins_d = DI("sins", [128, 128])
    mkp_d = DI("mkp", [NH * 128, 128]); mks_d = DI("mks", [NH * 128, 128])
    cdp_d = DI("cdp", [NH * 128, 512]); cds_d = DI("cds", [NH * 128, 128])
    sdec_d = DI("sdec", [128, 16]); seqm_d = DI("seqm", [128, 16])

    yp = DO("yp", [1024, DM]); ys = DO("ys", [TS, DM])
    lrup = DO("lrup", [16, 128]); convp = DO("convp", [3, DM]); retp = DO("retp", [NH * 256, 512])
    lrus = DO("lrus", [16, DM]); convs = DO("convs", [48, DM]); rets = DO("rets", [16 * NH * 256, 512])

    def SB(name, shape, dt=F32):
        return es.enter_context(nc.sbuf_tensor(name, shape, dt))

    x_fm = SB("x_fm", [128, 16, TP]); xB = [Buf("x%d" % i) for i in range(16)]
    u_bf = SB("u_bf", [128, 16, TP], BF16); uB = [Buf("u%d" % i) for i in range(16)]
    scr = SB("scr", [128, 48, TP], BF16); sB = [Buf("s%d" % i) for i in range(48)]
    NSB = 4
    S_buf = SB("S_buf", [128, NSB, 2, 512]); SbB = [Buf("Sbuf%d" % i) for i in range(NSB)]
    sscr = nc.dram_tensor("sscr", [NH * 256, 512], F32, kind="Internal").ap()
    sscrB = [Buf("sscr%d" % h) for h in range(NH)]
    NSLOT = 5
    wring = SB("wring", [128, NSLOT, 16, 256], BF16); wB = [Buf("w%d" % i) for i in range(NSLOT)]
    NA = 16
    arena = SB("arena", [128, NA, 512]); aB = [Buf("a%d" % i) for i in range(NA)]
    arena_b = arena.bitcast(BF16)
    xe = SB("xe", [128, 2, 516]); xeB = [Buf("xe0"), Buf("xe1")]
    vecs = SB("vecs_sb", [128, NV]); vB = Buf("vecs")
    cl = SB("cl", [128, 16]); clB = Buf("cl")
    halo = SB("halo", [128, 16, 3]); haloB = Buf("halo")
    hprev = SB("hprev", [128, 16]); hpB = Buf("hprev")
    h0fm = SB("h0fm", [128, 16, 16]); h0B = Buf("h0fm")
    identf = SB("identf_sb", [128, 128]); idB = Buf("identf")
    identb = SB("identb", [128, 128], BF16); idbB = Buf("identb")
    onesb = SB("onesb", [128, 128], BF16); onB = Buf("onesb")
    bvb = SB("bvb", [1, 512], BF16); bvB = Buf("bvb")
    bvf = SB("bvf", [1, 512]); bvfB = Buf("bvf")
    cos_sb = SB("cos_sb", [128, TP]); sin_sb = SB("sin_sb", [128, TP]); csB = Buf("cossin")
    mk_sb = SB("mk_sb", [128, 2, 128]); mkB = [Buf("mk0"), Buf("mk1")]
    cd_sb = SB("cd_sb", [128, 1, TP]); cdB = [Buf("cd0"), Buf("cd0b")]
    sdec = SB("sdec_sb", [128, 16]); sdB = Buf("sdec")
    flag = SB("flag_sb", [128, 2]); flB = Buf("flag")
    seqm = SB("seqm_sb", [128, 16]); sqB = Buf("seqm")

    banks = [es.enter_context(nc.psum_tensor("bank%d" % i, [128, 512], F32)) for i in range(7)]
    bankB = [Buf("bank%d" % i) for i in range(7)]
    bank7 = es.enter_context(nc.psum_tensor("bank7", [128, 1024], BF16)); b7B = Buf("bank7")

    S = Sched(nc, es)
    st = {"bank": 0, "slot": 0, "alt": 0, "sbuf": 0}

    def nbank(avoid=None):
        i = st["bank"] % 7
        st["bank"] += 1
        if avoid is not None and banks[i] is avoid:
            i = st["bank"] % 7
            st["bank"] += 1
        return banks[i], bankB[i]

    def nsbuf():
        i = st["sbuf"] % NSB
        st["sbuf"] += 1
        return S_buf[:, i], SbB[i]

    def alt():
        st["alt"] += 1
        return "act" if st["alt"] % 2 else "dve"

    def A(i, n=1):
        return arena[:, i:i + n, :]

    def Ab(i):
        return arena_b[:, i, :]

    def slab(wd, r0, nk, c0, ncols=256):
        i = st["slot"] % NSLOT
        st["slot"] += 1
        src = wd[r0:r0 + nk * 128, c0:c0 + ncols].rearrange("(kc p) n -> p kc n", p=128)
        S.dma("pool", wring[:, i, 0:nk, 0:ncols], src, writes=[wB[i]])
        return wring[:, i], wB[i]

    def mm_group(outap, outB, pairs, rbufs, start=True, stop=True):
        n = len(pairs)

        def fn(pe):
            ins = None
            for j, (l, r) in enumerate(pairs):
                ins = pe.matmul(outap, l, r, start=(start and j == 0), stop=(stop and j == n - 1))
            return ins
        S.op("pe", fn, reads=rbufs, writes=[outB])

    def transposes(outs, ins_, ident, identB, rbufs, outB):
        def fn(pe):
            ins = None
            for o, i in zip(outs, ins_):
                k = i.shape[0]
                ins = pe.transpose(o, i, ident[0:k, 0:k])
            return ins
        S.op("pe", fn, reads=list(rbufs) + [identB], writes=[outB])

    S.dma("sp", vecs[:], vecs_d[:, :], writes=[vB])
    S.dma("sp", identf[:], identf_d[:, :], writes=[idB])
    S.dma("sp", sdec[:], sdec_d[:, :], writes=[sdB])
    S.dma("sp", flag[:], flag_d[:, :], writes=[flB])
    S.dma("sp", seqm[:], seqm_d[:, :], writes=[sqB])
    S.op("dve", lambda e: e.tensor_copy(out=identb[:], in_=identf[:]), reads=[idB], writes=[idbB])
    S.op("dve", lambda e: e.memset(onesb[:], 1.0), writes=[onB])
    S.op("dve", lambda e: e.memset(halo[:], 0.0), writes=[haloB])
    S.op("dve", lambda e: e.memset(hprev[:], 0.0), writes=[hpB])
    S.op("act", lambda e: e.activation(out=cl[:], in_=vecs[:, V_LAM:V_LAM + 16], func=AF.Exp, scale=-1.0),
         reads=[vB], writes=[clB])
    S.op("act", lambda e: e.activation(out=cl[:], in_=cl[:], func=AF.Ln, bias=1.0, scale=1.0),
         reads=[clB], writes=[clB])
    S.op("dve", lambda e: e.tensor_scalar_mul(out=cl[:], in0=cl[:], scalar1=-8.0), reads=[clB], writes=[clB])

    def vcol(c):
        return vecs[:, c:c + 1]

    def load_x(srcs):
        for tc, src in enumerate(srcs):
            a0 = 4 * (tc % 2)
            xin = arena[:, a0:a0 + 4, :]
            xinB = aB[a0:a0 + 4]
            S.dma("sp", xin, src.rearrange("t (a b) -> t a b", a=4), writes=xinB)
            for g in range(4):
                bk, bB = nbank()
                transposes([bk[:, j * 128:(j + 1) * 128] for j in range(4)],
                           [arena[:, a0 + g, j * 128:(j + 1) * 128] for j in range(4)],
                           identf, idB, [xinB[g]], bB)
                e = alt()
                dst = x_fm[:, 4 * g:4 * g + 4, tc * 128:(tc + 1) * 128]
                srcv = bk[:, :].rearrange("p (j t) -> p j t", j=4)
                if e == "act":
                    S.op("act", lambda en, d=dst, s_=srcv: en.copy(out=d, in_=s_), reads=[bB], writes=xB[4 * g:4 * g + 4])
                else:
                    S.op("dve", lambda en, d=dst, s_=srcv: en.tensor_copy(out=d, in_=s_), reads=[bB], writes=xB[4 * g:4 * g + 4])

    def rstd_of(src_fn, nchunks, srcBs, T, inv_n, sq_slot=12, rs_slot=13):
        bk, bB = nbank()
        for kc in range(nchunks):
            half = kc % 2
            sq = arena_b[:, sq_slot, half * 512:half * 512 + T]
            S.op("act", lambda en, o=sq, i=src_fn(kc): en.activation(out=o, in_=i, func=AF.Square),
                 reads=[srcBs[kc]], writes=[aB[sq_slot]])
            mm_group(bk[:, 0:T], bB, [(onesb[:, :], sq)], [onB, aB[sq_slot]], start=(kc == 0), stop=(kc == nchunks - 1))
        rs = arena[:, rs_slot, 0:T]
        S.op("act", lambda en: en.activation(out=rs, in_=bk[:, 0:T], func=AF.Sqrt, scale=inv_n, bias=EPS),
             reads=[bB], writes=[aB[rs_slot]])
        S.op("dve", lambda en: en.reciprocal(out=rs, in_=rs), reads=[aB[rs_slot]], writes=[aB[rs_slot]])
        return rs, aB[rs_slot]

    def rmsnorm_u(gcol, T):
        rs, rsB = rstd_of(lambda kc: x_fm[:, kc, 0:T], 16, xB, T, 1.0 / DM)
        for kc in range(16):
            S.op("dve", lambda en, kc=kc: en.scalar_tensor_tensor(
                out=u_bf[:, kc, 0:T], in0=x_fm[:, kc, 0:T], scalar=vcol(gcol + kc), in1=rs,
                op0=ALU.mult, op1=ALU.mult), reads=[xB[kc], rsB, vB], writes=[uB[kc]])

    def fm_proj(wd, c0, rhs_fn, rhsBs, nk_total, T, consume, r0=0):
        nsl = (nk_total + 15) // 16
        bks = [nbank() for _ in range(2)]
        for s_ in range(nsl):
            k0 = s_ * 16
            nk = min(16, nk_total - k0)
            sl, slB = slab(wd, r0 + k0 * 128, nk, c0)
            for j in range(2):
                bk, bB = bks[j]
                mm_group(bk[:, 0:T], bB,
                         [(sl[:, kc, j * 128:(j + 1) * 128], rhs_fn(k0 + kc)) for kc in range(nk)],
                         [slB] + [rhsBs[k0 + kc] for kc in range(nk)],
                         start=(s_ == 0), stop=(s_ == nsl - 1))
        for j in range(2):
            consume(j, bks[j][0], bks[j][1])

    def u_rhs(T):
        return lambda kc: u_bf[:, kc, 0:T]

    def ffn(wg, wu, wdn, gcol, T):
        rmsnorm_u(gcol, T)
        for hp in range(DFF // 256):
            gb = {}

            def cg(j, bk, bB):
                gb[j] = (bk, bB)
            fm_proj(W[wg], hp * 256, u_rhs(T), uB, 16, T, cg)
            ub = {}

            def cu(j, bk, bB):
                ub[j] = (bk, bB)
            fm_proj(W[wu], hp * 256, u_rhs(T), uB, 16, T, cu)
            for j in range(2):
                hc = hp * 2 + j
                sl_ = 10 + (hc % 2)
                sg = arena[:, sl_, 0:T]
                S.op("act", lambda en, o=sg, i=gb[j][0][:, 0:T]: en.activation(out=o, in_=i, func=AF.Silu),
                     reads=[gb[j][1]], writes=[aB[sl_]])
                S.op("dve", lambda en, o=scr[:, hc, 0:T], a=sg, b=ub[j][0][:, 0:T]: en.tensor_tensor(
                    out=o, in0=a, in1=b, op=ALU.mult), reads=[aB[sl_], ub[j][1]], writes=[sB[hc]])
        for og in range(DM // 256):
            def cd_(j, bk, bB, og=og):
                oc = og * 2 + j
                S.op("dve", lambda en: en.scalar_tensor_tensor(
                    out=x_fm[:, oc, 0:T], in0=bk[:, 0:T], scalar=0.5, in1=x_fm[:, oc, 0:T],
                    op0=ALU.mult, op1=ALU.add), reads=[bB, xB[oc]], writes=[xB[oc]])
            fm_proj(W[wdn], og * 256, lambda kc: scr[:, kc, 0:T], sB, DFF // 128, T, cd_)

    def evac_bias(dst, bk_ap, bcol, rB, wBs, eng="act"):
        if eng == "act":
            S.op("act", lambda en: en.activation(out=dst, in_=bk_ap, func=AF.Identity, bias=vcol(bcol), scale=1.0),
                 reads=[rB, vB], writes=wBs)
        else:
            S.op("dve", lambda en: en.tensor_scalar(out=dst, in0=bk_ap, scalar1=vcol(bcol), scalar2=None,
                                                     op0=ALU.add), reads=[rB, vB], writes=wBs)

    XS = 336

    def mixer(nP, hasS, state_only=False, first_mode=None, last=False, cs=None, s_mode=None):
        TPc = nP * 128
        so = TPc
        T = TPc + (128 if hasS else 0)
        ntc = T // 128
        NS = 16
        assert (not hasS) or 3 + TPc <= XS
        rmsnorm_u(V_MIX, T)
        if hasS:
            cin_flat = arena[:, 12:14, :].rearrange("p a b -> p (a b)")
            for g in range(4):
                S.dma("sp", arena[0:48, 11, :], sconv[:, g * 512:(g + 1) * 512], writes=[aB[11]])
                bk, bB = nbank()
                transposes([bk[:, j * 48:(j + 1) * 48] for j in range(4)],
                           [arena[0:48, 11, j * 128:(j + 1) * 128] for j in range(4)], identf, idB, [aB[11]], bB)
                S.op("dve", lambda en, g=g, bk=bk: en.tensor_copy(out=cin_flat[:, g * 192:(g + 1) * 192], in_=bk[:, 0:192]),
                     reads=[bB], writes=[aB[12], aB[13]])
            cin = cin_flat[:, 0:768].rearrange("p (c s r) -> p c s r", c=16, s=16)
            for g in range(4):
                S.dma("sp", arena[0:16, 11, :], slru[:, g * 512:(g + 1) * 512], writes=[aB[11]])
                bk, bB = nbank()
                transposes([bk[:, j * 16:(j + 1) * 16] for j in range(4)],
                           [arena[0:16, 11, j * 128:(j + 1) * 128] for j in range(4)], identf, idB, [aB[11]], bB)
                S.op("dve", lambda en, g=g, bk=bk: en.tensor_copy(
                    out=h0fm[:, 4 * g:4 * g + 4, :], in_=bk[:, 0:64].rearrange("p (c s) -> p c s", c=4)),
                    reads=[bB], writes=[h0B])

        def xes(j):
            return xe[:, j, XS:XS + 176].rearrange("p (s t) -> p s t", t=11)

        def s3(ap):
            return ap.rearrange("p (s t) -> p s t", t=8)

        hh_store = {}

        def A_xa(n):
            def c_xa(j, bk, bB, n=n):
                c = 2 * n + j
                if nP:
                    evac_bias(xe[:, j, 3:3 + TPc], bk[:, 0:TPc], V_BIN + c, bB, [xeB[j]], "act")
                    S.op("dve", lambda en: en.tensor_copy(out=xe[:, j, 0:3], in_=halo[:, c, :]),
                         reads=[haloB], writes=[xeB[j]])
                if hasS:
                    evac_bias(xes(j)[:, :, 3:11], s3(bk[:, so:so + 128]), V_BIN + c, bB, [xeB[j]], "act")
                    S.op("dve", lambda en: en.tensor_copy(out=xes(j)[:, :, 0:3], in_=cin[:, c]),
                         reads=[aB[12], aB[13]], writes=[xeB[j]])
            fm_proj(W["w_in"], C_XA + n * 256, u_rhs(T), uB, 16, T, c_xa)

        def xs_(n, j):
            return j if n % 2 == 0 else 14 + j

        def A_conv(n):
            for j in range(2):
                c = 2 * n + j
                xsl = xs_(n, j)
                parts = []
                if nP:
                    parts.append((arena[:, xsl, 0:TPc], lambda k, j=j: xe[:, j, k:k + TPc]))
                if hasS:
                    parts.append((s3(arena[:, xsl, so:so + 128]), lambda k, j=j: xes(j)[:, :, k:k + 8]))
                for xc_, sh in parts:
                    S.op("dve", lambda en: en.tensor_scalar(out=xc_, in0=sh(0), scalar1=vcol(V_CW + c), scalar2=vcol(V_CB + c),
                                                             op0=ALU.mult, op1=ALU.add), reads=[xeB[j], vB], writes=[aB[xsl]])
                    for k in range(1, 4):
                        S.op("dve", lambda en, k=k: en.scalar_tensor_tensor(
                            out=xc_, in0=sh(k), scalar=vcol(V_CW + 16 * k + c), in1=xc_, op0=ALU.mult, op1=ALU.add),
                            reads=[xeB[j], vB, aB[xsl]], writes=[aB[xsl]])
                S.op("act", lambda en, j=j, xsl=xsl: en.copy(out=arena_b[:, 2, j * 512:j * 512 + T], in_=arena[:, xsl, 0:T]),
                     reads=[aB[xsl]], writes=[aB[2]])
                if nP:
                    S.op("dve", lambda en, j=j, c=c: en.tensor_copy(out=halo[:, c, :], in_=xe[:, j, TPc:TPc + 3]),
                         reads=[xeB[j]], writes=[haloB])
            tails = []
            if hasS:
                tails.append((48, True))
            if last:
                tails.append((3, False))
            for R, is_s in tails:
                for j in range(2):
                    ctmp = arena[:, 10, j * 64:j * 64 + R]
                    if is_s:
                        S.op("dve", lambda en, j=j, ctmp=ctmp: en.tensor_copy(
                            out=ctmp.rearrange("p (s r) -> p s r", r=3), in_=xes(j)[:, :, 8:11]),
                            reads=[xeB[j]], writes=[aB[10]])
                    else:
                        S.op("dve", lambda en, j=j, ctmp=ctmp: en.tensor_copy(out=ctmp, in_=xe[:, j, TPc:TPc + 3]),
                             reads=[xeB[j]], writes=[aB[10]])
                bk, bB = nbank()
                transposes([bk[0:R, j * 128:(j + 1) * 128] for j in range(2)],
                           [arena[:, 10, j * 64:j * 64 + R] for j in range(2)], identf, idB, [aB[10]], bB)
                S.op("act", lambda en, bk=bk, R=R: en.copy(out=arena[0:R, 11, 0:256], in_=bk[0:R, 0:256]),
                     reads=[bB], writes=[aB[11]])
                S.dma("sp", (convs if is_s else convp)[:, n * 256:(n + 1) * 256], arena[0:R, 11, 0:256],
                      reads=[aB[11]], is_out=True)
        def A_gates(n):
            for wi, (wname, bcolbase, aslot) in enumerate((("lru_wa", V_BA, 3), ("lru_wx", V_BX, 4))):
                def cgate(j, bk, bB, aslot=aslot, bcolbase=bcolbase, n=n):
                    c = 2 * n + j
                    dst = arena[:, aslot + 2 * j, 0:T]
                    S.op("act", lambda en: en.activation(out=dst, in_=bk[:, 0:T], func=AF.Sigmoid,
                                                          bias=vcol(bcolbase + c), scale=1.0),
                         reads=[bB, vB], writes=[aB[aslot + 2 * j]])
                fm_proj(W[wname], 0, lambda kc: arena_b[:, 2, kc * 512:kc * 512 + T], [aB[2], aB[2]], 2, T, cgate, r0=n * 256)
        def A_chain(n):
            hhs = {}
            hh_store[n] = hhs
            for j in range(2):
                c = 2 * n + j
                a_ = arena[:, 3 + 2 * j, 0:T]; aBj = aB[3 + 2 * j]
                gi = arena[:, 4 + 2 * j, 0:T]; giB = aB[4 + 2 * j]
                m_ = arena[:, 7, 0:T]; mB = aB[7]
                xc = arena[:, xs_(n, j), 0:T]; xcB = aB[xs_(n, j)]
                S.op("act", lambda en: en.activation(out=a_, in_=a_, func=AF.Exp, scale=cl[:, c:c + 1]),
                     reads=[aBj, clB], writes=[aBj])
                S.op("dve", lambda en: en.tensor_tensor(out=m_, in0=a_, in1=a_, op=ALU.mult), reads=[aBj], writes=[mB])
                S.op("act", lambda en: en.activation(out=m_, in_=m_, func=AF.Sqrt, scale=-1.0, bias=1.0),
                     reads=[mB], writes=[mB])
                if first_mode == "always":
                    S.op("dve", lambda en: en.memset(m_[:, 0:1], 1.0), reads=[], writes=[mB])
                elif first_mode == "flagA":
                    S.op("dve", lambda en: en.tensor_scalar(out=m_[:, 0:1], in0=m_[:, 0:1], scalar1=flag[:, 0:1],
                                                             scalar2=flag[:, 1:2], op0=ALU.mult, op1=ALU.add),
                         reads=[mB, flB], writes=[mB])
                S.op("dve", lambda en: en.tensor_tensor(out=gi, in0=gi, in1=xc, op=ALU.mult), reads=[giB, xcB], writes=[giB])
                S.op("dve", lambda en: en.tensor_tensor(out=gi, in0=gi, in1=m_, op=ALU.mult), reads=[giB, mB], writes=[giB])
                hh = arena[:, 8, 0:T]; hhB = aB[8]
                if hasS:
                    a3 = s3(a_[:, so:so + 128])
                    g3 = s3(gi[:, so:so + 128])
                    tmp = arena[:, 10, 256:272]
                    S.op("dve", lambda en: en.tensor_tensor(out=tmp, in0=a3[:, :, 0], in1=h0fm[:, c, :], op=ALU.mult),
                         reads=[aBj, h0B], writes=[aB[10]])
                    S.op("dve", lambda en: en.tensor_tensor(out=g3[:, :, 0], in0=g3[:, :, 0], in1=tmp, op=ALU.add),
                         reads=[giB, aB[10]], writes=[giB])
                    S.op("dve", lambda en: en.memset(a3[:, :, 0], 0.0), reads=[], writes=[aBj])
                if nP:
                    S.op("dve", lambda en: en.tensor_tensor_scan(out=hh, data0=a_, data1=gi, initial=hprev[:, c:c + 1],
                                                                  op0=ALU.mult, op1=ALU.add),
                         reads=[aBj, giB, hpB], writes=[hhB])
                    S.op("dve", lambda en: en.tensor_copy(out=hprev[:, c:c + 1], in_=hh[:, TPc - 1:TPc]),
                         reads=[hhB], writes=[hpB])
                else:
                    S.op("dve", lambda en: en.tensor_tensor_scan(out=hh, data0=a_, data1=gi, initial=0.0,
                                                                  op0=ALU.mult, op1=ALU.add),
                         reads=[aBj, giB], writes=[hhB])
                if hasS:
                    S.op("dve", lambda en: en.tensor_copy(out=arena[:, 10, 320 + j * 16:320 + (j + 1) * 16],
                                                          in_=s3(hh[:, so:so + 128])[:, :, 7]),
                         reads=[hhB], writes=[aB[10]])
                hhs[j] = (hh, hhB)
                if j == 0 and not state_only:
                    keep = arena[:, 9, 0:T]
                    S.op("act", lambda en, keep=keep, hh=hh: en.copy(out=keep, in_=hh), reads=[hhB], writes=[aB[9]])
                    hhs[0] = (keep, aB[9])
            if hasS:
                bk, bB = nbank()
                transposes([bk[0:16, j * 128:(j + 1) * 128] for j in range(2)],
                           [arena[:, 10, 320 + j * 16:320 + (j + 1) * 16] for j in range(2)], identf, idB, [aB[10]], bB)
                S.op("act", lambda en, bk=bk: en.copy(out=arena[0:16, 11, 256:512], in_=bk[0:16, 0:256]),
                     reads=[bB], writes=[aB[11]])
                S.dma("sp", lrus[:, n * 256:(n + 1) * 256], arena[0:16, 11, 256:512], reads=[aB[11]], is_out=True)

        def A_ga(n):
            hhs = hh_store[n]

            def c_ga(j, bk, bB, n=n, hhs=hhs):
                c = 2 * n + j
                hh, hhB = hhs[j]
                xg = arena[:, 3, 0:T]; t_ = arena[:, 4, 0:T]
                evac_bias(xg, bk[:, 0:T], V_BIN + 16 + c, bB, [aB[3]], "act")
                S.op("dve", lambda en: en.tensor_tensor(out=t_, in0=xg, in1=xg, op=ALU.mult), reads=[aB[3]], writes=[aB[4]])
                S.op("dve", lambda en: en.tensor_scalar(out=t_, in0=t_, scalar1=0.044715, scalar2=1.0,
                                                         op0=ALU.mult, op1=ALU.add), reads=[aB[4]], writes=[aB[4]])
                S.op("dve", lambda en: en.tensor_tensor(out=t_, in0=t_, in1=xg, op=ALU.mult), reads=[aB[4], aB[3]], writes=[aB[4]])
                S.op("act", lambda en: en.activation(out=t_, in_=t_, func=AF.Sigmoid, scale=1.5957691216057308),
                     reads=[aB[4]], writes=[aB[4]])
                S.op("dve", lambda en: en.tensor_tensor(out=t_, in0=t_, in1=xg, op=ALU.mult), reads=[aB[4], aB[3]], writes=[aB[4]])
                S.op("dve", lambda en: en.tensor_tensor(out=scr[:, c, 0:T], in0=t_, in1=hh, op=ALU.mult),
                     reads=[aB[4], hhB], writes=[sB[c]])
            fm_proj(W["w_in"], C_GA + n * 256, u_rhs(T), uB, 16, T, c_ga)

        A_xa(0)
        A_conv(0)
        for n in range(8):
            A_gates(n)
            if n + 1 < 8:
                A_xa(n + 1)
                A_conv(n + 1)
            A_chain(n)
            if not state_only:
                A_ga(n)
        if last:
            bk, bB = nbank()
            transposes([bk[0:16, 0:128]], [hprev[:, 0:16]], identf, idB, [hpB], bB)
            S.op("act", lambda en: en.copy(out=arena[0:16, 11, 256:384], in_=bk[0:16, 0:128]), reads=[bB], writes=[aB[11]])
            S.dma("sp", lrup[:, :], arena[0:16, 11, 256:384], reads=[aB[11]], is_out=True)

        for og in range(0 if state_only else 8):
            pb_ = {}

            def c_pa(j, bk, bB):
                pb_[j] = (bk, bB)
            fm_proj(W["proj_a"], og * 256, lambda kc: scr[:, kc, 0:T], sB, 16, T, c_pa)

            def c_gta(j, bk, bB, og=og):
                oc = og * 2 + j
                sl_ = 10 + (oc % 2)
                sg = arena[:, sl_, 0:T]
                S.op("act", lambda en: en.activation(out=sg, in_=bk[:, 0:T], func=AF.Sigmoid,
                                                      bias=vcol(V_BIN + C_GTA // 128 + oc), scale=1.0),
                     reads=[bB, vB], writes=[aB[sl_]])
                S.op("dve", lambda en: en.tensor_tensor(out=scr[:, 32 + oc, 0:T], in0=sg, in1=pb_[j][0][:, 0:T], op=ALU.mult),
                     reads=[aB[sl_], pb_[j][1]], writes=[sB[32 + oc]])
            fm_proj(W["w_in"], C_GTA + og * 256, u_rhs(T), uB, 16, T, c_gta)

        for (c0_, n_, cd_, sd_) in cs:
            S.dma("sp", cos_sb[:, c0_:c0_ + n_], cd_, writes=[csB])
            S.dma("sp", sin_sb[:, c0_:c0_ + n_], sd_, writes=[csB])
        for h in range(NH):
            gam = 1.0 - 2.0 ** (-5.0 - h)
            if not state_only:
                if nP:
                    S.dma("sp", mk_sb[:, 0, :], mkp_d[h * 128:(h + 1) * 128, :], writes=[mkB[0]])
                    S.dma("sp", cd_sb[:, 0, 0:TPc], cdp_d[h * 128:(h + 1) * 128, 0:TPc], writes=[cdB[0]])
                if hasS:
                    S.dma("sp", mk_sb[:, 1, :], mks_d[h * 128:(h + 1) * 128, :], writes=[mkB[1]])
                    S.dma("sp", cd_sb[:, 0, so:so + 128], cds_d[h * 128:(h + 1) * 128, :], writes=[cdB[0]])
            S.dma("sp", bvf[:, :], bv_d[:, h * 512:(h + 1) * 512], writes=[bvfB])
            S.op("dve", lambda en: en.tensor_copy(out=bvb[:, :], in_=bvf[:, :]), reads=[bvfB], writes=[bvB])
            qf = arena[:, 0:2, :]; kf = arena[:, 2:4, :]

            def c_q(j, bk, bB, h=h):
                S.op("dve", lambda en: en.scalar_tensor_tensor(
                    out=qf[:, j, 0:T], in0=bk[:, 0:T], scalar=vcol(V_BIN + C_Q // 128 + 2 * h + j), in1=cd_sb[:, 0, 0:T],
                    op0=ALU.add, op1=ALU.mult), reads=[bB, vB, cdB[0]], writes=[aB[j]])
            if not state_only:
                fm_proj(W["w_in"], C_Q + h * 256, u_rhs(T), uB, 16, T, c_q)

            def c_k(j, bk, bB, h=h):
                evac_bias(kf[:, j, 0:T], bk[:, 0:T], V_BIN + C_K // 128 + 2 * h + j, bB, [aB[2 + j]], "act")
            fm_proj(W["w_in"], C_K + h * 256, u_rhs(T), uB, 16, T, c_k)
            vtm = lambda tc: arena_b[:, 10 + tc // 2, (tc % 2) * 512:(tc % 2) * 512 + 512]
            vtmB = lambda tc: aB[10 + tc // 2]
            vb = [nbank() for _ in range(ntc)]
            for s_ in range(2):
                c0 = C_V + h * 512 + s_ * 256
                sl, slB = slab(W["w_in"], 0, 16, c0)
                for tc in range(ntc):
                    bk, bB = vb[tc]
                    pairs = [(u_bf[:, kc, tc * 128:(tc + 1) * 128], sl[:, kc, 0:256]) for kc in range(16)]
                    pairs.append((onesb[0:1, 0:128], bvb[0:1, s_ * 256:(s_ + 1) * 256]))
                    mm_group(bk[:, s_ * 256:(s_ + 1) * 256], bB, pairs, [slB, onB, bvB] + uB)
            for tc in range(ntc):
                S.op("act", lambda en, tc=tc: en.copy(out=vtm(tc), in_=vb[tc][0][:, :]), reads=[vb[tc][1]], writes=[vtmB(tc)])
            for (src, sBs, dslot) in (((kf, [aB[2], aB[3]], 8),) if state_only else ((qf, [aB[0], aB[1]], 6), (kf, [aB[2], aB[3]], 8))):
                t1 = arena[:, 4, 0:T]; t2 = arena[:, 5, 0:T]
                x1 = src[:, 0, 0:T]; x2 = src[:, 1, 0:T]
                d1 = arena_b[:, dslot, 0:T]; d2 = arena_b[:, dslot, 512:512 + T]
                S.op("dve", lambda en, x1=x1: en.tensor_tensor(out=t1, in0=x1, in1=cos_sb[:, 0:T], op=ALU.mult),
                     reads=[sBs[0], csB], writes=[aB[4]])
                S.op("dve", lambda en, x2=x2: en.tensor_tensor(out=t2, in0=x2, in1=sin_sb[:, 0:T], op=ALU.mult),
                     reads=[sBs[1], csB], writes=[aB[5]])
                S.op("dve", lambda en, d1=d1: en.tensor_tensor(out=d1, in0=t1, in1=t2, op=ALU.subtract),
                     reads=[aB[4], aB[5]], writes=[aB[dslot]])
                S.op("dve", lambda en, x2=x2: en.tensor_tensor(out=t1, in0=x2, in1=cos_sb[:, 0:T], op=ALU.mult),
                     reads=[sBs[1], csB], writes=[aB[4]])
                S.op("dve", lambda en, x1=x1: en.tensor_tensor(out=t2, in0=x1, in1=sin_sb[:, 0:T], op=ALU.mult),
                     reads=[sBs[0], csB], writes=[aB[5]])
                S.op("dve", lambda en, d2=d2: en.tensor_tensor(out=d2, in0=t1, in1=t2, op=ALU.add),
                     reads=[aB[4], aB[5]], writes=[aB[dslot]])
            qdT = lambda kc, lo, n_: arena_b[:, 6, kc * 512 + lo:kc * 512 + lo + n_]
            kT = lambda kc, lo, n_: arena_b[:, 8, kc * 512 + lo:kc * 512 + lo + n_]
            ktm = arena_b[:, 9, :].rearrange("p (t f) -> p t f", f=256)
            for tc in range(ntc):
                sc_ = h if tc < nP else 8 + h
                transposes([bank7[:, kc * 128:(kc + 1) * 128] for kc in range(2)],
                           [kT(kc, tc * 128, 128) for kc in range(2)], identb, idbB, [aB[8]], b7B)
                S.op("dve", lambda en, tc=tc, sc_=sc_: en.tensor_scalar(
                    out=ktm[:, tc, :], in0=bank7[:, 0:256], scalar1=sdec[:, sc_:sc_ + 1],
                    scalar2=None, op0=ALU.mult), reads=[b7B, sdB], writes=[aB[9]])
            o_sb = arena[:, 0:4, :]
            if nP:
                Sc, ScB = nsbuf()
                if s_mode == "zero":
                    S.op("dve", lambda en: en.memset(Sc.rearrange("p k v -> p (k v)"), 0.0), writes=[ScB])
                else:
                    S.dma("sp", Sc, sscr[h * 256:(h + 1) * 256, :].rearrange("(k p) v -> p k v", p=128),
                          reads=[sscrB[h]], writes=[ScB])
                    if s_mode == "flag":
                        S.op("dve", lambda en: en.tensor_scalar(out=Sc.rearrange("p k v -> p (k v)"),
                                                                 in0=Sc.rearrange("p k v -> p (k v)"),
                                                                 scalar1=flag[:, 0:1], scalar2=None, op0=ALU.mult),
                             reads=[flB, ScB], writes=[ScB])
            def store_state(h=h):
                S.dma("sp", sscr[h * 256:(h + 1) * 256, :].rearrange("(k p) v -> p k v", p=128), Sc,
                      reads=[ScB], writes=[sscrB[h]])
                if last:
                    S.dma("sp", retp[h * 256:(h + 1) * 256, :].rearrange("(k p) v -> p k v", p=128), Sc,
                          reads=[ScB], is_out=True)

            for tc in range(ntc):
                is_s = tc >= nP
                if state_only:
                    cdec = float(gam ** 128)
                    for kc in range(2):
                        bk, bB = nbank()
                        mm_group(bk[:, :], bB, [(ktm[:, tc, kc * 128:(kc + 1) * 128], vtm(tc))], [aB[9], vtmB(tc)])
                        S.op("dve", lambda en, kc=kc, bk=bk: en.scalar_tensor_tensor(
                            out=Sc[:, kc, :], in0=Sc[:, kc, :], scalar=cdec, in1=bk[:, :],
                            op0=ALU.mult, op1=ALU.add), reads=[bB, ScB], writes=[ScB])
                    if tc == nP - 1:
                        store_state()
                    continue
                bk, bB = nbank()
                mm_group(bk[:, 0:128], bB, [(kT(kc, tc * 128, 128), qdT(kc, tc * 128, 128)) for kc in range(2)],
                         [aB[8], aB[6]])
                PT = arena_b[:, 13, 0:128]
                mi = 1 if is_s else 0
                S.op("dve", lambda en, bk=bk, mi=mi: en.tensor_tensor(out=PT, in0=bk[:, 0:128], in1=mk_sb[:, mi, :], op=ALU.mult),
                     reads=[bB, mkB[mi]], writes=[aB[13]])
                ob_, obB = nbank()
                if not is_s:
                    Sb = arena_b[:, 12, :].rearrange("p (k v) -> p k v", k=2)
                    S.op("act", lambda en: en.copy(out=Sb, in_=Sc), reads=[ScB], writes=[aB[12]])
                    for vc in range(4):
                        pairs = [(vtm(tc)[:, vc * 128:(vc + 1) * 128], PT)]
                        pairs += [(Sb[:, kc, vc * 128:(vc + 1) * 128], qdT(kc, tc * 128, 128)) for kc in range(2)]
                        mm_group(ob_[:, vc * 128:(vc + 1) * 128], obB, pairs, [vtmB(tc), aB[13], aB[12], aB[6]])
                else:
                    for vc in range(4):
                        mm_group(ob_[:, vc * 128:(vc + 1) * 128], obB, [(vtm(tc)[:, vc * 128:(vc + 1) * 128], PT)],
                                 [vtmB(tc), aB[13]], start=(vc == 0), stop=False)
                    cdec_s = float(gam ** 8)
                    PF = 3
                    pend = []

                    def s0_load(sq, h=h):
                        rr = (sq * NH + h) * 256
                        buf, bufB = nsbuf()
                        S.dma("sp", buf, sret[rr:rr + 256, :].rearrange("(k p) v -> p k v", p=128), writes=[bufB])
                        pend.append((buf, bufB))
                    for sq in range(min(PF, NS)):
                        s0_load(sq)
                    for s_ in range(NS):
                        r0 = (s_ * NH + h) * 256
                        S0f, S0fB = pend.pop(0)
                        S0b = arena_b[:, 7, :].rearrange("p (k v) -> p k v", k=2)
                        S.op("act", lambda en, S0f=S0f: en.copy(out=S0b, in_=S0f), reads=[S0fB], writes=[aB[7]])
                        for vc in range(4):
                            mm_group(ob_[:, vc * 128 + s_ * 8:vc * 128 + s_ * 8 + 8], obB,
                                     [(S0b[:, kc, vc * 128:(vc + 1) * 128], qdT(kc, tc * 128 + s_ * 8, 8)) for kc in range(2)],
                                     [aB[7], aB[6]], start=False, stop=(s_ == NS - 1))
                        km = arena_b[:, 12, 0:256]
                        S.op("dve", lambda en, s_=s_, tc=tc: en.tensor_scalar(out=km, in0=ktm[:, tc, :], scalar1=seqm[:, s_:s_ + 1],
                                                                               scalar2=None, op0=ALU.mult),
                             reads=[aB[9], sqB], writes=[aB[12]])
                        for kc in range(2):
                            bk, bB = nbank(avoid=ob_)
                            mm_group(bk[:, :], bB, [(km[:, kc * 128:(kc + 1) * 128], vtm(tc))], [aB[12], vtmB(tc)])
                            S.op("dve", lambda en, kc=kc, bk=bk, S0f=S0f: en.scalar_tensor_tensor(
                                out=S0f[:, kc, :], in0=S0f[:, kc, :], scalar=cdec_s, in1=bk[:, :],
                                op0=ALU.mult, op1=ALU.add), reads=[bB, S0fB], writes=[S0fB])
                        S.dma("sp", rets[r0:r0 + 256, :].rearrange("(k p) v -> p k v", p=128), S0f, reads=[S0fB], is_out=True)
                        if s_ + PF < NS:
                            s0_load(s_ + PF)
                e = alt()
                dst = o_sb[:, :, tc * 128:(tc + 1) * 128]
                srcv = ob_[:, :].rearrange("p (v t) -> p v t", v=4)
                if e == "act":
                    S.op("act", lambda en: en.copy(out=dst, in_=srcv), reads=[obB], writes=aB[0:4])
                else:
                    S.op("dve", lambda en: en.tensor_copy(out=dst, in_=srcv), reads=[obB], writes=aB[0:4])
                if not is_s:
                    cdec = float(gam ** 128)
                    for kc in range(2):
                        bk, bB = nbank()
                        mm_group(bk[:, :], bB, [(ktm[:, tc, kc * 128:(kc + 1) * 128], vtm(tc))], [aB[9], vtmB(tc)])
                        S.op("dve", lambda en, kc=kc, bk=bk: en.scalar_tensor_tensor(
                            out=Sc[:, kc, :], in0=Sc[:, kc, :], scalar=cdec, in1=bk[:, :],
                            op0=ALU.mult, op1=ALU.add), reads=[bB, ScB], writes=[ScB])
                    if tc == nP - 1:
                        store_state()
            if state_only:
                continue
            rs, rsB = rstd_of(lambda vc: o_sb[:, vc, 0:T], 4, aB[0:4], T, 1.0 / 512.0, sq_slot=12, rs_slot=5)
            for vc in range(4):
                S.op("dve", lambda en, vc=vc: en.scalar_tensor_tensor(
                    out=o_sb[:, vc, 0:T], in0=o_sb[:, vc, 0:T], scalar=vcol(V_RN + h * 4 + vc), in1=rs,
                    op0=ALU.mult, op1=ALU.mult), reads=[aB[vc], rsB, vB], writes=[aB[vc]])
            for s_ in range(2):
                def c_gr(j, bk, bB, s_=s_, h=h):
                    vc = s_ * 2 + j
                    sg = arena[:, 4, 0:T]
                    S.op("act", lambda en: en.activation(out=sg, in_=bk[:, 0:T], func=AF.Silu,
                                                          bias=vcol(V_BIN + C_GR // 128 + h * 4 + vc), scale=1.0),
                         reads=[bB, vB], writes=[aB[4]])
                    S.op("dve", lambda en: en.tensor_tensor(out=scr[:, h * 4 + vc, 0:T], in0=sg, in1=o_sb[:, vc, 0:T], op=ALU.mult),
                         reads=[aB[4], aB[vc]], writes=[sB[h * 4 + vc]])
                fm_proj(W["w_in"], C_GR + h * 512 + s_ * 256, u_rhs(T), uB, 16, T, c_gr)
        for og in range(0 if state_only else 8):
            pb_ = {}

            def c_pb(j, bk, bB):
                pb_[j] = (bk, bB)
            fm_proj(W["proj_b"], og * 256, lambda kc: scr[:, kc, 0:T], sB, 32, T, c_pb)

            def c_gtb(j, bk, bB, og=og):
                oc = og * 2 + j
                sl_ = 10 + (oc % 2)
                sg = arena[:, sl_, 0:T]
                S.op("act", lambda en: en.activation(out=sg, in_=bk[:, 0:T], func=AF.Sigmoid,
                                                      bias=vcol(V_BIN + C_GTB // 128 + oc), scale=1.0),
                     reads=[bB, vB], writes=[aB[sl_]])
                S.op("dve", lambda en: en.tensor_tensor(out=sg, in0=sg, in1=pb_[j][0][:, 0:T], op=ALU.mult),
                     reads=[aB[sl_], pb_[j][1]], writes=[aB[sl_]])
                S.op("dve", lambda en: en.tensor_tensor(out=scr[:, 32 + oc, 0:T], in0=sg, in1=scr[:, 32 + oc, 0:T], op=ALU.add),
                     reads=[aB[sl_], sB[32 + oc]], writes=[sB[32 + oc]])
            fm_proj(W["w_in"], C_GTB + og * 256, u_rhs(T), uB, 16, T, c_gtb)
        for og in range(0 if state_only else 8):
            def c_wo(j, bk, bB, og=og):
                oc = og * 2 + j
                S.op("dve", lambda en: en.tensor_tensor(out=x_fm[:, oc, 0:T], in0=x_fm[:, oc, 0:T], in1=bk[:, 0:T], op=ALU.add),
                     reads=[bB, xB[oc]], writes=[xB[oc]])
            fm_proj(W["w_out"], og * 256, lambda kc: scr[:, 32 + kc, 0:T], sB[32:48], 16, T, c_wo)

    def ple_and_out(psrcs, ydsts):
        T = 128 * len(psrcs)
        rmsnorm_u(V_PLE, T)
        for tc, psrc in enumerate(psrcs):
            pin = arena[:, 8, 0:256]
            S.dma("sp", pin, psrc, writes=[aB[8]])
            bk, bB = nbank()
            transposes([bk[:, j * 128:(j + 1) * 128] for j in range(2)],
                       [arena[:, 8, j * 128:(j + 1) * 128] for j in range(2)], identf, idB, [aB[8]], bB)
            S.op("act", lambda en, tc=tc, bk=bk: en.copy(
                out=arena_b[:, 9, :].rearrange("p (k t) -> p k t", k=2)[:, :, tc * 128:(tc + 1) * 128],
                in_=bk[:, 0:256].rearrange("p (k t) -> p k t", k=2)), reads=[bB], writes=[aB[9]])
        for og in range(8):
            pb_ = {}

            def c_pp(j, bk, bB):
                pb_[j] = (bk, bB)
            fm_proj(W["ple_proj"], og * 256, lambda kc: arena_b[:, 9, kc * 512:kc * 512 + T], [aB[9], aB[9]], 2, T, c_pp)

            def c_pg(j, bk, bB, og=og):
                oc = og * 2 + j
                sl_ = 10 + (oc % 2)
                sg = arena[:, sl_, 0:T]
                S.op("act", lambda en: en.activation(out=sg, in_=bk[:, 0:T], func=AF.Sigmoid, bias=vcol(V_PBG + oc), scale=1.0),
                     reads=[bB, vB], writes=[aB[sl_]])
                S.op("dve", lambda en: en.tensor_tensor(out=sg, in0=sg, in1=pb_[j][0][:, 0:T], op=ALU.mult),
                     reads=[aB[sl_], pb_[j][1]], writes=[aB[sl_]])
                S.op("dve", lambda en: en.tensor_tensor(out=x_fm[:, oc, 0:T], in0=x_fm[:, oc, 0:T], in1=sg, op=ALU.add),
                     reads=[aB[sl_], xB[oc]], writes=[xB[oc]])
            fm_proj(W["ple_wg"], og * 256, u_rhs(T), uB, 16, T, c_pg)
        rs, rsB = rstd_of(lambda kc: x_fm[:, kc, 0:T], 16, xB, T, 1.0 / DM)
        for kc in range(16):
            S.op("dve", lambda en, kc=kc: en.scalar_tensor_tensor(
                out=x_fm[:, kc, 0:T], in0=x_fm[:, kc, 0:T], scalar=vcol(V_FIN + kc), in1=rs,
                op0=ALU.mult, op1=ALU.mult), reads=[xB[kc], rsB, vB], writes=[xB[kc]])
        for tc, ydst in enumerate(ydsts):
            a0 = 4 * (tc % 2)
            for g in range(4):
                bk, bB = nbank()
                transposes([bk[:, j * 128:(j + 1) * 128] for j in range(4)],
                           [x_fm[:, 4 * g + j, tc * 128:(tc + 1) * 128] for j in range(4)], identf, idB,
                           xB[4 * g:4 * g + 4], bB)
                e = alt()
                if e == "act":
                    S.op("act", lambda en, g=g, bk=bk: en.copy(out=arena[:, a0 + g, :], in_=bk[:, :]), reads=[bB], writes=[aB[a0 + g]])
                else:
                    S.op("dve", lambda en, g=g, bk=bk: en.tensor_copy(out=arena[:, a0 + g, :], in_=bk[:, :]), reads=[bB], writes=[aB[a0 + g]])
            S.dma("sp", ydst.rearrange("t (a b) -> t a b", a=4),
                  arena[:, a0:a0 + 4, :], reads=aB[a0:a0 + 4], is_out=True)

    def rows(d, r0, n=128):
        return d[r0:r0 + n, :]

    for ti in range(2):
        load_x([rows(xq, ti * 512 + k * 128) for k in range(4)])
        ffn("ffn1_wg", "ffn1_wu", "ffn1_wd", V_FFN1, 512)
        mixer(4, False, state_only=True, first_mode=("always" if ti == 0 else None), s_mode=("zero" if ti == 0 else None),
              cs=[(0, 512, cosq_d[:, ti * 512:(ti + 1) * 512], sinq_d[:, ti * 512:(ti + 1) * 512])])
    S.op("dve", lambda en: en.tensor_scalar(out=hprev[:, :], in0=hprev[:, :], scalar1=flag[:, 0:1], scalar2=None, op0=ALU.mult),
         reads=[flB, hpB], writes=[hpB])
    S.op("dve", lambda en: en.tensor_scalar(out=halo[:].rearrange("p c r -> p (c r)"), in0=halo[:].rearrange("p c r -> p (c r)"),
                                             scalar1=flag[:, 0:1], scalar2=None, op0=ALU.mult),
         reads=[flB, haloB], writes=[haloB])
    for ti in range(3):
        nP = 3 if ti < 2 else 2
        hasS = (ti == 2)
        T = 384
        r0 = ti * 384
        xsrc = [rows(xp, r0 + k * 128) for k in range(nP)] + ([rows(xs, 0)] if hasS else [])
        psrc = [rows(pp, r0 + k * 128) for k in range(nP)] + ([rows(ps, 0)] if hasS else [])
        ydst = [rows(yp, r0 + k * 128) for k in range(nP)] + ([rows(ys, 0)] if hasS else [])
        cs = [(0, nP * 128, cosp_d[:, r0:r0 + nP * 128], sinp_d[:, r0:r0 + nP * 128])]
        if hasS:
            cs.append((nP * 128, 128, coss_d[:, :], sins_d[:, :]))
        load_x(xsrc)
        ffn("ffn1_wg", "ffn1_wu", "ffn1_wd", V_FFN1, T)
        mixer(nP, hasS, first_mode=("flagA" if ti == 0 else None), last=(ti == 2), cs=cs,
              s_mode=("flag" if ti == 0 else None))
        ffn("ffn2_wg", "ffn2_wu", "ffn2_wd", V_FFN2, T)
        ple_and_out(psrc, ydst)
    S.finish()
    return nc, es


def _consts():
    f32 = np.float32
    h = np.arange(NH)
    gam = 1.0 - np.exp2(-5.0 - h)
    inv = (10000.0 ** (-(np.arange(128, dtype=f32)) / f32(128))).astype(f32)
    posp = np.arange(2048, dtype=f32)
    angp = (posp[None, :] * inv[:, None]).astype(f32)
    poss = (16384 + (np.arange(128) % 8)).astype(f32)
    angs = (poss[None, :] * inv[:, None]).astype(f32)
    c = {}
    c["identf"] = np.eye(128, dtype=f32)
    c["cos_all"] = np.cos(angp).astype(f32); c["sin_all"] = np.sin(angp).astype(f32)
    c["coss"] = np.cos(angs).astype(f32); c["sins"] = np.sin(angs).astype(f32)
    j = np.arange(128)[:, None]; i = np.arange(128)[None, :]
    mkp = np.zeros((NH, 128, 128)); mks = np.zeros((NH, 128, 128))
    cdp = np.zeros((NH, 128, 512)); cds = np.zeros((NH, 128, 128))
    sdec = np.zeros((128, 16))
    for hh in range(NH):
        g = gam[hh]
        mkp[hh] = np.where(i >= j, g ** (-(j + 1.0)), 0.0) / 16.0
        tj = j % 8; ti = i % 8
        mks[hh] = np.where((j // 8 == i // 8) & (ti >= tj), g ** (-(tj + 1.0)), 0.0) / 16.0
        cdp[hh] = (g ** ((np.arange(512) % 128) + 1.0))[None, :]
        cds[hh] = (g ** ((np.arange(128) % 8) + 1.0))[None, :]
        sdec[:, hh] = g ** (127.0 - np.arange(128)) / 16.0
        sdec[:, 8 + hh] = g ** (7.0 - (np.arange(128) % 8)) / 16.0
    c["mkp"] = mkp.reshape(NH * 128, 128).astype(f32); c["mks"] = mks.reshape(NH * 128, 128).astype(f32)
    c["cdp"] = cdp.reshape(NH * 128, 512).astype(f32); c["cds"] = cds.reshape(NH * 128, 128).astype(f32)
    c["sdec"] = sdec.astype(f32)
    c["seqm"] = (np.arange(128)[:, None] // 8 == np.arange(16)[None, :]).astype(f32)
    return c


def _fm(v):
    return np.ascontiguousarray(np.asarray(v, dtype=np.float32).reshape(-1, 128).T)


_CACHE = {}


def kernel(**inp):
    f32 = np.float32
    g = {k: np.asarray(v) for k, v in inp.items()}
    vecs = np.concatenate([
        _fm(g["ffn1_norm"][0]), _fm(g["mix_norm"][0]), _fm(g["ffn2_norm"][0]), _fm(g["ple_norm"][0]),
        _fm(g["final_norm"]), _fm(g["b_in"][0]),
        _fm(g["conv_w"][0][0]), _fm(g["conv_w"][0][1]), _fm(g["conv_w"][0][2]), _fm(g["conv_w"][0][3]),
        _fm(g["conv_b"][0]), _fm(g["lru_ba"][0]), _fm(g["lru_bx"][0]), _fm(g["lru_lambda"][0]),
        _fm(g["ret_norm"][0].reshape(-1)), _fm(g["ple_bg"][0])], axis=1).astype(f32)
    assert vecs.shape == (128, NV), vecs.shape
    shared = dict(_consts())
    cos_all = shared.pop("cos_all"); sin_all = shared.pop("sin_all")
    shared["cosq"] = np.ascontiguousarray(cos_all[:, 0:1024]); shared["sinq"] = np.ascontiguousarray(sin_all[:, 0:1024])
    shared["vecs"] = vecs
    shared["bv"] = np.ascontiguousarray(g["b_in"][0][C_V:C_V + 4096].reshape(1, 4096))
    for nm in ("ffn1_wg", "ffn1_wu", "ffn1_wd", "w_in", "proj_a", "proj_b", "w_out",
               "ffn2_wg", "ffn2_wu", "ffn2_wd", "ple_wg", "ple_proj"):
        shared[nm] = g[nm][0]
    shared["lru_wa"] = g["lru_wa"][0].reshape(NH * 256, 256)
    shared["lru_wx"] = g["lru_wx"][0].reshape(NH * 256, 256)
    in_maps = []
    for c in range(NCORES):
        b = c % 4
        hf = c // 4
        m = dict(shared)
        m["xp"] = g["x_prompt"][b, hf * 1024:(hf + 1) * 1024]
        m["xq"] = g["x_prompt"][b, 0:1024]
        m["cosp"] = np.ascontiguousarray(cos_all[:, hf * 1024:(hf + 1) * 1024])
        m["sinp"] = np.ascontiguousarray(sin_all[:, hf * 1024:(hf + 1) * 1024])
        fl = np.zeros((128, 2), np.float32); fl[:, 0] = float(hf); fl[:, 1] = 1.0 - float(hf)
        m["flag"] = fl
        m["xs"] = g["x_sample"][16 * c:16 * c + 16].reshape(TS, DM)
        m["pp"] = g["p_prompt"][0, b, hf * 1024:(hf + 1) * 1024]
        m["ps"] = g["p_sample"][0, 16 * c:16 * c + 16].reshape(TS, 256)
        m["slru"] = g["state_lru"][0, 16 * c:16 * c + 16]
        m["sconv"] = g["state_conv"][0, 16 * c:16 * c + 16].reshape(48, DM)
        m["sret"] = g["state_ret"][0, 16 * c:16 * c + 16].reshape(16 * NH * 256, 512)
        in_maps.append(m)
    if "nc" not in _CACHE:
        _CACHE["nc"] = build_nc()
    nc, _es = _CACHE["nc"]
    res = run_bass_kernel_spmd(nc, in_maps, core_ids=list(range(NCORES)))
    R = res.results
    y_prompt = np.stack([np.concatenate([R[b]["yp"], R[b + 4]["yp"]]) for b in range(4)]).astype(f32)
    y_sample = np.concatenate([R[c]["ys"].reshape(16, 8, DM) for c in range(NCORES)]).astype(f32)
    lru_p = np.stack([R[b + 4]["lrup"].reshape(DM) for b in range(4)])[None].astype(f32)
    conv_p = np.stack([R[b + 4]["convp"] for b in range(4)])[None].astype(f32)
    ret_p = np.stack([R[b + 4]["retp"].reshape(NH, 256, 512) for b in range(4)])[None].astype(f32)
    lru_s = np.concatenate([R[c]["lrus"] for c in range(NCORES)])[None].astype(f32)
    conv_s = np.concatenate([R[c]["convs"].reshape(16, 3, DM) for c in range(NCORES)])[None].astype(f32)
    ret_s = np.concatenate([R[c]["rets"].reshape(16, NH, 256, 512) for c in range(NCORES)])[None].astype(f32)
    return (y_prompt, y_sample, lru_p, conv_p, ret_p, lru_s, conv_s, ret_s)
```

```python
import numpy as np
import concourse.bass as bass
import concourse.mybir as mybir
from concourse.bass_utils import run_bass_kernel_spmd

F32 = mybir.dt.float32
BF16 = mybir.dt.bfloat16
AF = mybir.ActivationFunctionType
ALU = mybir.AluOpType

NCORES = 8
DM = 2048
DFF = 5632
NH = 8
EPS = 1e-6
TP = 512
TS = 128
NPT = 1024 // TP

V_FFN1, V_MIX, V_FFN2, V_PLE, V_FIN, V_BIN = 0, 16, 32, 48, 64, 80
V_CW, V_CB, V_BA, V_BX, V_LAM, V_RN, V_PBG = 240, 304, 320, 336, 352, 368, 400
NV = 416
C_XA, C_GA, C_Q, C_K, C_V, C_GR, C_GTA, C_GTB = 0, 2048, 4096, 6144, 8192, 12288, 16384, 18432

SAME_ENGINE_SYNC = True


class Buf:
    __slots__ = ("name", "lastw", "readers")

    def __init__(self, name):
        self.name = name
        self.lastw = None
        self.readers = {}


class Sched:
    def __init__(self, nc, es):
        self.nc = nc
        self.eng = {"pe": nc.tensor, "act": nc.scalar, "dve": nc.vector, "pool": nc.gpsimd, "sp": nc.sync}
        self.sem = {}
        self.cnt = {}
        self.seen = {k: {} for k in self.eng}
        for k in ("pe", "act", "dve"):
            self.sem[k] = es.enter_context(nc.semaphore("sem_" + k))
            self.cnt[k] = 0
        self.dsems = {"sp": [], "pool": []}
        for q, n in (("sp", 16), ("pool", 8)):
            for i in range(n):
                key = "d_%s_%d" % (q, i)
                self.sem[key] = es.enter_context(nc.semaphore(key))
                self.cnt[key] = 0
                self.dsems[q].append(key)
        self.drr = {"sp": 0, "pool": 0}
        self.out_tags = []

    def _wait(self, e, deps):
        for k, v in deps.items():
            if k == e and (e == "pe" or not SAME_ENGINE_SYNC):
                continue
            if self.seen[e].get(k, 0) < v:
                self.eng[e].wait_ge(self.sem[k], v)
                self.seen[e][k] = v

    @staticmethod
    def _add(deps, tag):
        if tag is not None:
            k, v = tag
            if deps.get(k, 0) < v:
                deps[k] = v

    def _deps(self, reads, writes):
        deps = {}
        for b in reads:
            self._add(deps, b.lastw)
        for b in writes:
            self._add(deps, b.lastw)
            for k, v in b.readers.items():
                self._add(deps, (k, v))
        return deps

    def _commit(self, tag, reads, writes):
        k, v = tag
        for b in reads:
            if b.readers.get(k, 0) < v:
                b.readers[k] = v
        for b in writes:
            b.lastw = tag
            b.readers = {}

    def op(self, e, fn, reads=(), writes=()):
        self._wait(e, self._deps(reads, writes))
        ins = fn(self.eng[e])
        self.cnt[e] += 1
        ins.then_inc(self.sem[e], 1)
        self._commit((e, self.cnt[e]), reads, writes)

    def dma(self, q, out, in_, reads=(), writes=(), is_out=False):
        key = self.dsems[q][self.drr[q] % len(self.dsems[q])]
        self.drr[q] += 1
        deps = self._deps(reads, writes)
        if self.cnt[key] > 0:
            self._add(deps, (key, self.cnt[key]))
        self._wait(q, deps)
        self.cnt[key] += 16
        self.eng[q].dma_start(out=out, in_=in_).then_inc(self.sem[key], 16)
        tag = (key, self.cnt[key])
        self._commit(tag, reads, writes)
        if is_out:
            self.out_tags.append(tag)

    def finish(self):
        deps = {}
        for t in self.out_tags:
            self._add(deps, t)
        self._wait("sp", deps)


def build_nc():
    from contextlib import ExitStack
    nc = bass.Bass("TRN2", target_bir_lowering=False)
    es = ExitStack()

    def DI(name, shape):
        return nc.dram_tensor(name, shape, F32, kind="ExternalInput").ap()

    def DO(name, shape):
        return nc.dram_tensor(name, shape, F32, kind="ExternalOutput").ap()

    xp = DI("xp", [1024, DM]); xq = DI("xq", [1024, DM]); xs = DI("xs", [TS, DM])
    pp = DI("pp", [1024, 256]); ps = DI("ps", [TS, 256])
    slru = DI("slru", [16, DM]); sconv = DI("sconv", [48, DM]); sret = DI("sret", [16 * NH * 256, 512])
    W = {}
    for nm, shp in (("ffn1_wg", [DM, DFF]), ("ffn1_wu", [DM, DFF]), ("ffn1_wd", [DFF, DM]),
                    ("w_in", [DM, 20480]), ("lru_wa", [NH * 256, 256]), ("lru_wx", [NH * 256, 256]),
                    ("proj_a", [DM, DM]), ("proj_b", [4096, DM]), ("w_out", [DM, DM]),
                    ("ffn2_wg", [DM, DFF]), ("ffn2_wu", [DM, DFF]), ("ffn2_wd", [DFF, DM]),
                    ("ple_wg", [DM, DM]), ("ple_proj", [256, DM])):
        W[nm] = DI(nm, shp)
    vecs_d = DI("vecs", [128, NV]); bv_d = DI("bv", [1, 4096])
    identf_d = DI("identf", [128, 128])
    cosp_d = DI("cosp", [128, 1024]); sinp_d = DI("sinp", [128, 1024])
    cosq_d = DI("cosq", [128, 1024]); sinq_d = DI("sinq", [128, 1024])
    flag_d = DI("flag", [128, 2])
    coss_d = DI("coss", [128, 128]); sins_d = DI("sins", [128, 128])
    mkp_d = DI("mkp", [NH * 128, 128]); mks_d = DI("mks", [NH * 128, 128])
    cdp_d = DI("cdp", [NH * 128, 512]); cds_d = DI("cds", [NH * 128, 128])
    sdec_d = DI("sdec", [128, 16]); seqm_d = DI("seqm", [128, 16])

    yp = DO("yp", [1024, DM]); ys = DO("ys", [TS, DM])
    lrup = DO("lrup", [16, 128]); convp = DO("convp", [3, DM]); retp = DO("retp", [NH * 256, 512])
    lrus = DO("lrus", [16, DM]); convs = DO("convs", [48, DM]); rets = DO("rets", [16 * NH * 256, 512])

    def SB(name, shape, dt=F32):
        return es.enter_context(nc.sbuf_tensor(name, shape, dt))

    x_fm = SB("x_fm", [128, 16, TP]); xB = [Buf("x%d" % i) for i in range(16)]
    u_bf = SB("u_bf", [128, 16, TP], BF16); uB = [Buf("u%d" % i) for i in range(16)]
    scr = SB("scr", [128, 48, TP], BF16); sB = [Buf("s%d" % i) for i in range(48)]
    NSB = 4
    S_buf = SB("S_buf", [128, NSB, 2, 512]); SbB = [Buf("Sbuf%d" % i) for i in range(NSB)]
    sscr = nc.dram_tensor("sscr", [NH * 256, 512], F32, kind="Internal").ap()
    sscrB = [Buf("sscr%d" % h) for h in range(NH)]
    NSLOT = 5
    wring = SB("wring", [128, NSLOT, 16, 256], BF16); wB = [Buf("w%d" % i) for i in range(NSLOT)]
    NA = 14
    arena = SB("arena", [128, NA, 512]); aB = [Buf("a%d" % i) for i in range(NA)]
    arena_b = arena.bitcast(BF16)
    xe = SB("xe", [128, 2, 516]); xeB = [Buf("xe0"), Buf("xe1")]
    vecs = SB("vecs_sb", [128, NV]); vB = Buf("vecs")
    cl = SB("cl", [128, 16]); clB = Buf("cl")
    halo = SB("halo", [128, 16, 3]); haloB = Buf("halo")
    hprev = SB("hprev", [128, 16]); hpB = Buf("hprev")
    h0fm = SB("h0fm", [128, 16, 16]); h0B = Buf("h0fm")
    identf = SB("identf_sb", [128, 128]); idB = Buf("identf")
    identb = SB("identb", [128, 128], BF16); idbB = Buf("identb")
    onesb = SB("onesb", [128, 128], BF16); onB = Buf("onesb")
    bvb = SB("bvb", [1, 512], BF16); bvB = Buf("bvb")
    bvf = SB("bvf", [1, 512]); bvfB = Buf("bvf")
    cos_sb = SB("cos_sb", [128, TP]); sin_sb = SB("sin_sb", [128, TP]); csB = Buf("cossin")
    mk_sb = SB("mk_sb", [128, 2, 128]); mkB = [Buf("mk0"), Buf("mk1")]
    cd_sb = SB("cd_sb", [128, 1, TP]); cdB = [Buf("cd0"), Buf("cd0b")]
    sdec = SB("sdec_sb", [128, 16]); sdB = Buf("sdec")
    flag = SB("flag_sb", [128, 2]); flB = Buf("flag")
    seqm = SB("seqm_sb", [128, 16]); sqB = Buf("seqm")

    banks = [es.enter_context(nc.psum_tensor("bank%d" % i, [128, 512], F32)) for i in range(7)]
    bankB = [Buf("bank%d" % i) for i in range(7)]
    bank7 = es.enter_context(nc.psum_tensor("bank7", [128, 1024], BF16)); b7B = Buf("bank7")

    S = Sched(nc, es)
    st = {"bank": 0, "slot": 0, "alt": 0, "sbuf": 0}

    def nbank(avoid=None):
        i = st["bank"] % 7
        st["bank"] += 1
        if avoid is not None and banks[i] is avoid:
            i = st["bank"] % 7
            st["bank"] += 1
        return banks[i], bankB[i]

    def nsbuf():
        i = st["sbuf"] % NSB
        st["sbuf"] += 1
        return S_buf[:, i], SbB[i]

    def alt():
        st["alt"] += 1
        return "act" if st["alt"] % 2 else "dve"

    def A(i, n=1):
        return arena[:, i:i + n, :]

    def Ab(i):
        return arena_b[:, i, :]

    def slab(wd, r0, nk, c0, ncols=256):
        i = st["slot"] % NSLOT
        st["slot"] += 1
        src = wd[r0:r0 + nk * 128, c0:c0 + ncols].rearrange("(kc p) n -> p kc n", p=128)
        S.dma("pool", wring[:, i, 0:nk, 0:ncols], src, writes=[wB[i]])
        return wring[:, i], wB[i]

    def mm_group(outap, outB, pairs, rbufs, start=True, stop=True):
        n = len(pairs)

        def fn(pe):
            ins = None
            for j, (l, r) in enumerate(pairs):
                ins = pe.matmul(outap, l, r, start=(start and j == 0), stop=(stop and j == n - 1))
            return ins
        S.op("pe", fn, reads=rbufs, writes=[outB])

    def transposes(outs, ins_, ident, identB, rbufs, outB):
        def fn(pe):
            ins = None
            for o, i in zip(outs, ins_):
                k = i.shape[0]
                ins = pe.transpose(o, i, ident[0:k, 0:k])
            return ins
        S.op("pe", fn, reads=list(rbufs) + [identB], writes=[outB])

    S.dma("sp", vecs[:], vecs_d[:, :], writes=[vB])
    S.dma("sp", identf[:], identf_d[:, :], writes=[idB])
    S.dma("sp", sdec[:], sdec_d[:, :], writes=[sdB])
    S.dma("sp", flag[:], flag_d[:, :], writes=[flB])
    S.dma("sp", seqm[:], seqm_d[:, :], writes=[sqB])
    S.op("dve", lambda e: e.tensor_copy(out=identb[:], in_=identf[:]), reads=[idB], writes=[idbB])
    S.op("dve", lambda e: e.memset(onesb[:], 1.0), writes=[onB])
    S.op("dve", lambda e: e.memset(halo[:], 0.0), writes=[haloB])
    S.op("dve", lambda e: e.memset(hprev[:], 0.0), writes=[hpB])
    S.op("act", lambda e: e.activation(out=cl[:], in_=vecs[:, V_LAM:V_LAM + 16], func=AF.Exp, scale=-1.0),
         reads=[vB], writes=[clB])
    S.op("act", lambda e: e.activation(out=cl[:], in_=cl[:], func=AF.Ln, bias=1.0, scale=1.0),
         reads=[clB], writes=[clB])
    S.op("dve", lambda e: e.tensor_scalar_mul(out=cl[:], in0=cl[:], scalar1=-8.0), reads=[clB], writes=[clB])

    def vcol(c):
        return vecs[:, c:c + 1]

    def load_x(srcs):
        for tc, src in enumerate(srcs):
            a0 = 4 * (tc % 2)
            xin = arena[:, a0:a0 + 4, :]
            xinB = aB[a0:a0 + 4]
            S.dma("sp", xin, src.rearrange("t (a b) -> t a b", a=4), writes=xinB)
            for g in range(4):
                bk, bB = nbank()
                transposes([bk[:, j * 128:(j + 1) * 128] for j in range(4)],
                           [arena[:, a0 + g, j * 128:(j + 1) * 128] for j in range(4)],
                           identf, idB, [xinB[g]], bB)
                e = alt()
                dst = x_fm[:, 4 * g:4 * g + 4, tc * 128:(tc + 1) * 128]
                srcv = bk[:, :].rearrange("p (j t) -> p j t", j=4)
                if e == "act":
                    S.op("act", lambda en, d=dst, s_=srcv: en.copy(out=d, in_=s_), reads=[bB], writes=xB[4 * g:4 * g + 4])
                else:
                    S.op("dve", lambda en, d=dst, s_=srcv: en.tensor_copy(out=d, in_=s_), reads=[bB], writes=xB[4 * g:4 * g + 4])

    def rstd_of(src_fn, nchunks, srcBs, T, inv_n, sq_slot=12, rs_slot=13, wide=False):
        bk, bB = nbank()
        if wide:
            for kc in range(nchunks):
                sq = scr[:, kc, 0:T]
                if kc % 2 == 0:
                    S.op("act", lambda en, o=sq, i=src_fn(kc): en.activation(out=o, in_=i, func=AF.Square),
                         reads=[srcBs[kc]], writes=[sB[kc]])
                else:
                    S.op("dve", lambda en, o=sq, i=src_fn(kc): en.tensor_tensor(out=o, in0=i, in1=i, op=ALU.mult),
                         reads=[srcBs[kc]], writes=[sB[kc]])
            for kc in range(nchunks):
                mm_group(bk[:, 0:T], bB, [(onesb[:, :], scr[:, kc, 0:T])], [onB, sB[kc]],
                         start=(kc == 0), stop=(kc == nchunks - 1))
        for kc in range(0 if wide else nchunks):
            half = kc % 2
            sq = arena_b[:, sq_slot, half * 512:half * 512 + T]
            S.op("act", lambda en, o=sq, i=src_fn(kc): en.activation(out=o, in_=i, func=AF.Square),
                 reads=[srcBs[kc]], writes=[aB[sq_slot]])
            mm_group(bk[:, 0:T], bB, [(onesb[:, :], sq)], [onB, aB[sq_slot]], start=(kc == 0), stop=(kc == nchunks - 1))
        rs = arena[:, rs_slot, 0:T]
        S.op("act", lambda en: en.activation(out=rs, in_=bk[:, 0:T], func=AF.Sqrt, scale=inv_n, bias=EPS),
             reads=[bB], writes=[aB[rs_slot]])
        S.op("dve", lambda en: en.reciprocal(out=rs, in_=rs), reads=[aB[rs_slot]], writes=[aB[rs_slot]])
        return rs, aB[rs_slot]

    def rmsnorm_u(gcol, T):
        rs, rsB = rstd_of(lambda kc: x_fm[:, kc, 0:T], 16, xB, T, 1.0 / DM, wide=True)
        for kc in range(16):
            S.op("dve", lambda en, kc=kc: en.scalar_tensor_tensor(
                out=u_bf[:, kc, 0:T], in0=x_fm[:, kc, 0:T], scalar=vcol(gcol + kc), in1=rs,
                op0=ALU.mult, op1=ALU.mult), reads=[xB[kc], rsB, vB], writes=[uB[kc]])

    def fm_proj(wd, c0, rhs_fn, rhsBs, nk_total, T, consume, r0=0):
        nsl = (nk_total + 15) // 16
        bks = [nbank() for _ in range(2)]
        for s_ in range(nsl):
            k0 = s_ * 16
            nk = min(16, nk_total - k0)
            sl, slB = slab(wd, r0 + k0 * 128, nk, c0)
            for j in range(2):
                bk, bB = bks[j]
                mm_group(bk[:, 0:T], bB,
                         [(sl[:, kc, j * 128:(j + 1) * 128], rhs_fn(k0 + kc)) for kc in range(nk)],
                         [slB] + [rhsBs[k0 + kc] for kc in range(nk)],
                         start=(s_ == 0), stop=(s_ == nsl - 1))
        for j in range(2):
            consume(j, bks[j][0], bks[j][1])

    def u_rhs(T):
        return lambda kc: u_bf[:, kc, 0:T]

    def ffn(wg, wu, wdn, gcol, T):
        rmsnorm_u(gcol, T)
        for hp in range(DFF // 256):
            gb = {}

            def cg(j, bk, bB):
                gb[j] = (bk, bB)
            fm_proj(W[wg], hp * 256, u_rhs(T), uB, 16, T, cg)
            ub = {}

            def cu(j, bk, bB):
                ub[j] = (bk, bB)
            fm_proj(W[wu], hp * 256, u_rhs(T), uB, 16, T, cu)
            for j in range(2):
                hc = hp * 2 + j
                sl_ = 10 + (hc % 2)
                sg = arena[:, sl_, 0:T]
                S.op("act", lambda en, o=sg, i=gb[j][0][:, 0:T]: en.activation(out=o, in_=i, func=AF.Silu),
                     reads=[gb[j][1]], writes=[aB[sl_]])
                S.op("dve", lambda en, o=scr[:, hc, 0:T], a=sg, b=ub[j][0][:, 0:T]: en.tensor_tensor(
                    out=o, in0=a, in1=b, op=ALU.mult), reads=[aB[sl_], ub[j][1]], writes=[sB[hc]])
        for og in range(DM // 256):
            def cd_(j, bk, bB, og=og):
                oc = og * 2 + j
                S.op("dve", lambda en: en.scalar_tensor_tensor(
                    out=x_fm[:, oc, 0:T], in0=bk[:, 0:T], scalar=0.5, in1=x_fm[:, oc, 0:T],
                    op0=ALU.mult, op1=ALU.add), reads=[bB, xB[oc]], writes=[xB[oc]])
            fm_proj(W[wdn], og * 256, lambda kc: scr[:, kc, 0:T], sB, DFF // 128, T, cd_)

    def evac_bias(dst, bk_ap, bcol, rB, wBs, eng="act"):
        if eng == "act":
            S.op("act", lambda en: en.activation(out=dst, in_=bk_ap, func=AF.Identity, bias=vcol(bcol), scale=1.0),
                 reads=[rB, vB], writes=wBs)
        else:
            S.op("dve", lambda en: en.tensor_scalar(out=dst, in0=bk_ap, scalar1=vcol(bcol), scalar2=None,
                                                     op0=ALU.add), reads=[rB, vB], writes=wBs)

    XS = 336

    def mixer(nP, hasS, state_only=False, first_mode=None, last=False, cs=None, s_mode=None):
        TPc = nP * 128
        so = TPc
        T = TPc + (128 if hasS else 0)
        ntc = T // 128
        NS = 16
        assert (not hasS) or 3 + TPc <= XS
        rmsnorm_u(V_MIX, T)
        if hasS:
            cin_flat = arena[:, 12:14, :].rearrange("p a b -> p (a b)")
            for g in range(4):
                S.dma("sp", arena[0:48, 11, :], sconv[:, g * 512:(g + 1) * 512], writes=[aB[11]])
                bk, bB = nbank()
                transposes([bk[:, j * 48:(j + 1) * 48] for j in range(4)],
                           [arena[0:48, 11, j * 128:(j + 1) * 128] for j in range(4)], identf, idB, [aB[11]], bB)
                S.op("dve", lambda en, g=g, bk=bk: en.tensor_copy(out=cin_flat[:, g * 192:(g + 1) * 192], in_=bk[:, 0:192]),
                     reads=[bB], writes=[aB[12], aB[13]])
            cin = cin_flat[:, 0:768].rearrange("p (c s r) -> p c s r", c=16, s=16)
            for g in range(4):
                S.dma("sp", arena[0:16, 11, :], slru[:, g * 512:(g + 1) * 512], writes=[aB[11]])
                bk, bB = nbank()
                transposes([bk[:, j * 16:(j + 1) * 16] for j in range(4)],
                           [arena[0:16, 11, j * 128:(j + 1) * 128] for j in range(4)], identf, idB, [aB[11]], bB)
                S.op("dve", lambda en, g=g, bk=bk: en.tensor_copy(
                    out=h0fm[:, 4 * g:4 * g + 4, :], in_=bk[:, 0:64].rearrange("p (c s) -> p c s", c=4)),
                    reads=[bB], writes=[h0B])

        def xes(j):
            return xe[:, j, XS:XS + 176].rearrange("p (s t) -> p s t", t=11)

        def s3(ap):
            return ap.rearrange("p (s t) -> p s t", t=8)

        hh_store = {}

        def A_xa(n):
            def c_xa(j, bk, bB, n=n):
                c = 2 * n + j
                if nP:
                    evac_bias(xe[:, j, 3:3 + TPc], bk[:, 0:TPc], V_BIN + c, bB, [xeB[j]], "act")
                    S.op("dve", lambda en: en.tensor_copy(out=xe[:, j, 0:3], in_=halo[:, c, :]),
                         reads=[haloB], writes=[xeB[j]])
                if hasS:
                    evac_bias(xes(j)[:, :, 3:11], s3(bk[:, so:so + 128]), V_BIN + c, bB, [xeB[j]], "act")
                    S.op("dve", lambda en: en.tensor_copy(out=xes(j)[:, :, 0:3], in_=cin[:, c]),
                         reads=[aB[12], aB[13]], writes=[xeB[j]])
            fm_proj(W["w_in"], C_XA + n * 256, u_rhs(T), uB, 16, T, c_xa)

        def A_conv(n):
            for j in range(2):
                c = 2 * n + j
                parts = []
                if nP:
                    parts.append((arena[:, j, 0:TPc], lambda k, j=j: xe[:, j, k:k + TPc]))
                if hasS:
                    parts.append((s3(arena[:, j, so:so + 128]), lambda k, j=j: xes(j)[:, :, k:k + 8]))
                for xc_, sh in parts:
                    S.op("dve", lambda en: en.tensor_scalar(out=xc_, in0=sh(0), scalar1=vcol(V_CW + c), scalar2=vcol(V_CB + c),
                                                             op0=ALU.mult, op1=ALU.add), reads=[xeB[j], vB], writes=[aB[j]])
                    for k in range(1, 4):
                        S.op("dve", lambda en, k=k: en.scalar_tensor_tensor(
                            out=xc_, in0=sh(k), scalar=vcol(V_CW + 16 * k + c), in1=xc_, op0=ALU.mult, op1=ALU.add),
                            reads=[xeB[j], vB, aB[j]], writes=[aB[j]])
                S.op("act", lambda en, j=j: en.copy(out=arena_b[:, 2, j * 512:j * 512 + T], in_=arena[:, j, 0:T]),
                     reads=[aB[j]], writes=[aB[2]])
                if nP:
                    S.op("dve", lambda en, j=j, c=c: en.tensor_copy(out=halo[:, c, :], in_=xe[:, j, TPc:TPc + 3]),
                         reads=[xeB[j]], writes=[haloB])
            tails = []
            if hasS:
                tails.append((48, True))
            if last:
                tails.append((3, False))
            for R, is_s in tails:
                for j in range(2):
                    ctmp = arena[:, 10, j * 64:j * 64 + R]
                    if is_s:
                        S.op("dve", lambda en, j=j, ctmp=ctmp: en.tensor_copy(
                            out=ctmp.rearrange("p (s r) -> p s r", r=3), in_=xes(j)[:, :, 8:11]),
                            reads=[xeB[j]], writes=[aB[10]])
                    else:
                        S.op("dve", lambda en, j=j, ctmp=ctmp: en.tensor_copy(out=ctmp, in_=xe[:, j, TPc:TPc + 3]),
                             reads=[xeB[j]], writes=[aB[10]])
                bk, bB = nbank()
                transposes([bk[0:R, j * 128:(j + 1) * 128] for j in range(2)],
                           [arena[:, 10, j * 64:j * 64 + R] for j in range(2)], identf, idB, [aB[10]], bB)
                S.op("act", lambda en, bk=bk, R=R: en.copy(out=arena[0:R, 11, 0:256], in_=bk[0:R, 0:256]),
                     reads=[bB], writes=[aB[11]])
                S.dma("sp", (convs if is_s else convp)[:, n * 256:(n + 1) * 256], arena[0:R, 11, 0:256],
                      reads=[aB[11]], is_out=True)
        def A_gates(n):
            for wi, (wname, bcolbase, aslot) in enumerate((("lru_wa", V_BA, 3), ("lru_wx", V_BX, 4))):
                def cgate(j, bk, bB, aslot=aslot, bcolbase=bcolbase, n=n):
                    c = 2 * n + j
                    dst = arena[:, aslot + 2 * j, 0:T]
                    S.op("act", lambda en: en.activation(out=dst, in_=bk[:, 0:T], func=AF.Sigmoid,
                                                          bias=vcol(bcolbase + c), scale=1.0),
                         reads=[bB, vB], writes=[aB[aslot + 2 * j]])
                fm_proj(W[wname], 0, lambda kc: arena_b[:, 2, kc * 512:kc * 512 + T], [aB[2], aB[2]], 2, T, cgate, r0=n * 256)
        def A_chain(n):
            hhs = {}
            hh_store[n] = hhs
            for j in range(2):
                c = 2 * n + j
                a_ = arena[:, 3 + 2 * j, 0:T]; aBj = aB[3 + 2 * j]
                gi = arena[:, 4 + 2 * j, 0:T]; giB = aB[4 + 2 * j]
                m_ = arena[:, 7, 0:T]; mB = aB[7]
                xc = arena[:, j, 0:T]; xcB = aB[j]
                S.op("act", lambda en: en.activation(out=a_, in_=a_, func=AF.Exp, scale=cl[:, c:c + 1]),
                     reads=[aBj, clB], writes=[aBj])
                S.op("dve", lambda en: en.tensor_tensor(out=m_, in0=a_, in1=a_, op=ALU.mult), reads=[aBj], writes=[mB])
                S.op("act", lambda en: en.activation(out=m_, in_=m_, func=AF.Sqrt, scale=-1.0, bias=1.0),
                     reads=[mB], writes=[mB])
                if first_mode == "always":
                    S.op("dve", lambda en: en.memset(m_[:, 0:1], 1.0), reads=[], writes=[mB])
                elif first_mode == "flagA":
                    S.op("dve", lambda en: en.tensor_scalar(out=m_[:, 0:1], in0=m_[:, 0:1], scalar1=flag[:, 0:1],
                                                             scalar2=flag[:, 1:2], op0=ALU.mult, op1=ALU.add),
                         reads=[mB, flB], writes=[mB])
                S.op("dve", lambda en: en.tensor_tensor(out=gi, in0=gi, in1=xc, op=ALU.mult), reads=[giB, xcB], writes=[giB])
                S.op("dve", lambda en: en.tensor_tensor(out=gi, in0=gi, in1=m_, op=ALU.mult), reads=[giB, mB], writes=[giB])
                hh = arena[:, 8, 0:T]; hhB = aB[8]
                if hasS:
                    a3 = s3(a_[:, so:so + 128])
                    g3 = s3(gi[:, so:so + 128])
                    tmp = arena[:, 10, 256:272]
                    S.op("dve", lambda en: en.tensor_tensor(out=tmp, in0=a3[:, :, 0], in1=h0fm[:, c, :], op=ALU.mult),
                         reads=[aBj, h0B], writes=[aB[10]])
                    S.op("dve", lambda en: en.tensor_tensor(out=g3[:, :, 0], in0=g3[:, :, 0], in1=tmp, op=ALU.add),
                         reads=[giB, aB[10]], writes=[giB])
                    S.op("dve", lambda en: en.memset(a3[:, :, 0], 0.0), reads=[], writes=[aBj])
                if nP:
                    S.op("dve", lambda en: en.tensor_tensor_scan(out=hh, data0=a_, data1=gi, initial=hprev[:, c:c + 1],
                                                                  op0=ALU.mult, op1=ALU.add),
                         reads=[aBj, giB, hpB], writes=[hhB])
                    S.op("dve", lambda en: en.tensor_copy(out=hprev[:, c:c + 1], in_=hh[:, TPc - 1:TPc]),
                         reads=[hhB], writes=[hpB])
                else:
                    S.op("dve", lambda en: en.tensor_tensor_scan(out=hh, data0=a_, data1=gi, initial=0.0,
                                                                  op0=ALU.mult, op1=ALU.add),
                         reads=[aBj, giB], writes=[hhB])
                if hasS:
                    S.op("dve", lambda en: en.tensor_copy(out=arena[:, 10, 320 + j * 16:320 + (j + 1) * 16],
                                                          in_=s3(hh[:, so:so + 128])[:, :, 7]),
                         reads=[hhB], writes=[aB[10]])
                hhs[j] = (hh, hhB)
                if j == 0 and not state_only:
                    keep = arena[:, 9, 0:T]
                    S.op("act", lambda en, keep=keep, hh=hh: en.copy(out=keep, in_=hh), reads=[hhB], writes=[aB[9]])
                    hhs[0] = (keep, aB[9])
            if hasS:
                bk, bB = nbank()
                transposes([bk[0:16, j * 128:(j + 1) * 128] for j in range(2)],
                           [arena[:, 10, 320 + j * 16:320 + (j + 1) * 16] for j in range(2)], identf, idB, [aB[10]], bB)
                S.op("act", lambda en, bk=bk: en.copy(out=arena[0:16, 11, 256:512], in_=bk[0:16, 0:256]),
                     reads=[bB], writes=[aB[11]])
                S.dma("sp", lrus[:, n * 256:(n + 1) * 256], arena[0:16, 11, 256:512], reads=[aB[11]], is_out=True)

        def A_ga(n):
            hhs = hh_store[n]

            def c_ga(j, bk, bB, n=n, hhs=hhs):
                c = 2 * n + j
                hh, hhB = hhs[j]
                xg = arena[:, 3, 0:T]; t_ = arena[:, 4, 0:T]
                evac_bias(xg, bk[:, 0:T], V_BIN + 16 + c, bB, [aB[3]], "act")
                S.op("dve", lambda en: en.tensor_tensor(out=t_, in0=xg, in1=xg, op=ALU.mult), reads=[aB[3]], writes=[aB[4]])
                S.op("dve", lambda en: en.tensor_scalar(out=t_, in0=t_, scalar1=0.044715, scalar2=1.0,
                                                         op0=ALU.mult, op1=ALU.add), reads=[aB[4]], writes=[aB[4]])
                S.op("dve", lambda en: en.tensor_tensor(out=t_, in0=t_, in1=xg, op=ALU.mult), reads=[aB[4], aB[3]], writes=[aB[4]])
                S.op("act", lambda en: en.activation(out=t_, in_=t_, func=AF.Sigmoid, scale=1.5957691216057308),
                     reads=[aB[4]], writes=[aB[4]])
                S.op("dve", lambda en: en.tensor_tensor(out=t_, in0=t_, in1=xg, op=ALU.mult), reads=[aB[4], aB[3]], writes=[aB[4]])
                S.op("dve", lambda en: en.tensor_tensor(out=scr[:, c, 0:T], in0=t_, in1=hh, op=ALU.mult),
                     reads=[aB[4], hhB], writes=[sB[c]])
            fm_proj(W["w_in"], C_GA + n * 256, u_rhs(T), uB, 16, T, c_ga)

        A_xa(0)
        A_conv(0)
        for n in range(8):
            if n + 1 < 8:
                A_xa(n + 1)
            A_gates(n)
            A_chain(n)
            if n + 1 < 8:
                A_conv(n + 1)
            if not state_only:
                A_ga(n)
        if last:
            bk, bB = nbank()
            transposes([bk[0:16, 0:128]], [hprev[:, 0:16]], identf, idB, [hpB], bB)
            S.op("act", lambda en: en.copy(out=arena[0:16, 11, 256:384], in_=bk[0:16, 0:128]), reads=[bB], writes=[aB[11]])
            S.dma("sp", lrup[:, :], arena[0:16, 11, 256:384], reads=[aB[11]], is_out=True)

        for og in range(0 if state_only else 8):
            pb_ = {}

            def c_pa(j, bk, bB):
                pb_[j] = (bk, bB)
            fm_proj(W["proj_a"], og * 256, lambda kc: scr[:, kc, 0:T], sB, 16, T, c_pa)

            def c_gta(j, bk, bB, og=og):
                oc = og * 2 + j
                sl_ = 10 + (oc % 2)
                sg = arena[:, sl_, 0:T]
                S.op("act", lambda en: en.activation(out=sg, in_=bk[:, 0:T], func=AF.Sigmoid,
                                                      bias=vcol(V_BIN + C_GTA // 128 + oc), scale=1.0),
                     reads=[bB, vB], writes=[aB[sl_]])
                S.op("dve", lambda en: en.tensor_tensor(out=scr[:, 32 + oc, 0:T], in0=sg, in1=pb_[j][0][:, 0:T], op=ALU.mult),
                     reads=[aB[sl_], pb_[j][1]], writes=[sB[32 + oc]])
            fm_proj(W["w_in"], C_GTA + og * 256, u_rhs(T), uB, 16, T, c_gta)

        for (c0_, n_, cd_, sd_) in cs:
            S.dma("sp", cos_sb[:, c0_:c0_ + n_], cd_, writes=[csB])
            S.dma("sp", sin_sb[:, c0_:c0_ + n_], sd_, writes=[csB])
        for h in range(NH):
            gam = 1.0 - 2.0 ** (-5.0 - h)
            if not state_only:
                if nP:
                    S.dma("sp", mk_sb[:, 0, :], mkp_d[h * 128:(h + 1) * 128, :], writes=[mkB[0]])
                    S.dma("sp", cd_sb[:, 0, 0:TPc], cdp_d[h * 128:(h + 1) * 128, 0:TPc], writes=[cdB[0]])
                if hasS:
                    S.dma("sp", mk_sb[:, 1, :], mks_d[h * 128:(h + 1) * 128, :], writes=[mkB[1]])
                    S.dma("sp", cd_sb[:, 0, so:so + 128], cds_d[h * 128:(h + 1) * 128, :], writes=[cdB[0]])
            S.dma("sp", bvf[:, :], bv_d[:, h * 512:(h + 1) * 512], writes=[bvfB])
            S.op("dve", lambda en: en.tensor_copy(out=bvb[:, :], in_=bvf[:, :]), reads=[bvfB], writes=[bvB])
            qf = arena[:, 0:2, :]; kf = arena[:, 2:4, :]

            def c_q(j, bk, bB, h=h):
                S.op("dve", lambda en: en.scalar_tensor_tensor(
                    out=qf[:, j, 0:T], in0=bk[:, 0:T], scalar=vcol(V_BIN + C_Q // 128 + 2 * h + j), in1=cd_sb[:, 0, 0:T],
                    op0=ALU.add, op1=ALU.mult), reads=[bB, vB, cdB[0]], writes=[aB[j]])
            if not state_only:
                fm_proj(W["w_in"], C_Q + h * 256, u_rhs(T), uB, 16, T, c_q)

            def c_k(j, bk, bB, h=h):
                evac_bias(kf[:, j, 0:T], bk[:, 0:T], V_BIN + C_K // 128 + 2 * h + j, bB, [aB[2 + j]], "act")
            fm_proj(W["w_in"], C_K + h * 256, u_rhs(T), uB, 16, T, c_k)
            vtm = lambda tc: arena_b[:, 10 + tc // 2, (tc % 2) * 512:(tc % 2) * 512 + 512]
            vtmB = lambda tc: aB[10 + tc // 2]
            vb = [nbank() for _ in range(ntc)]
            for s_ in range(2):
                c0 = C_V + h * 512 + s_ * 256
                sl, slB = slab(W["w_in"], 0, 16, c0)
                for tc in range(ntc):
                    bk, bB = vb[tc]
                    pairs = [(u_bf[:, kc, tc * 128:(tc + 1) * 128], sl[:, kc, 0:256]) for kc in range(16)]
                    pairs.append((onesb[0:1, 0:128], bvb[0:1, s_ * 256:(s_ + 1) * 256]))
                    mm_group(bk[:, s_ * 256:(s_ + 1) * 256], bB, pairs, [slB, onB, bvB] + uB)
            for tc in range(ntc):
                S.op("act", lambda en, tc=tc: en.copy(out=vtm(tc), in_=vb[tc][0][:, :]), reads=[vb[tc][1]], writes=[vtmB(tc)])
            for (src, sBs, dslot) in (((kf, [aB[2], aB[3]], 8),) if state_only else ((qf, [aB[0], aB[1]], 6), (kf, [aB[2], aB[3]], 8))):
                t1 = arena[:, 4, 0:T]; t2 = arena[:, 5, 0:T]
                x1 = src[:, 0, 0:T]; x2 = src[:, 1, 0:T]
                d1 = arena_b[:, dslot, 0:T]; d2 = arena_b[:, dslot, 512:512 + T]
                S.op("dve", lambda en, x1=x1: en.tensor_tensor(out=t1, in0=x1, in1=cos_sb[:, 0:T], op=ALU.mult),
                     reads=[sBs[0], csB], writes=[aB[4]])
                S.op("dve", lambda en, x2=x2: en.tensor_tensor(out=t2, in0=x2, in1=sin_sb[:, 0:T], op=ALU.mult),
                     reads=[sBs[1], csB], writes=[aB[5]])
                S.op("dve", lambda en, d1=d1: en.tensor_tensor(out=d1, in0=t1, in1=t2, op=ALU.subtract),
                     reads=[aB[4], aB[5]], writes=[aB[dslot]])
                S.op("dve", lambda en, x2=x2: en.tensor_tensor(out=t1, in0=x2, in1=cos_sb[:, 0:T], op=ALU.mult),
                     reads=[sBs[1], csB], writes=[aB[4]])
                S.op("dve", lambda en, x1=x1: en.tensor_tensor(out=t2, in0=x1, in1=sin_sb[:, 0:T], op=ALU.mult),
                     reads=[sBs[0], csB], writes=[aB[5]])
                S.op("dve", lambda en, d2=d2: en.tensor_tensor(out=d2, in0=t1, in1=t2, op=ALU.add),
                     reads=[aB[4], aB[5]], writes=[aB[dslot]])
            qdT = lambda kc, lo, n_: arena_b[:, 6, kc * 512 + lo:kc * 512 + lo + n_]
            kT = lambda kc, lo, n_: arena_b[:, 8, kc * 512 + lo:kc * 512 + lo + n_]
            ktm = arena_b[:, 9, :].rearrange("p (t f) -> p t f", f=256)
            for tc in range(ntc):
                sc_ = h if tc < nP else 8 + h
                transposes([bank7[:, kc * 128:(kc + 1) * 128] for kc in range(2)],
                           [kT(kc, tc * 128, 128) for kc in range(2)], identb, idbB, [aB[8]], b7B)
                S.op("dve", lambda en, tc=tc, sc_=sc_: en.tensor_scalar(
                    out=ktm[:, tc, :], in0=bank7[:, 0:256], scalar1=sdec[:, sc_:sc_ + 1],
                    scalar2=None, op0=ALU.mult), reads=[b7B, sdB], writes=[aB[9]])
            o_sb = arena[:, 0:4, :]
            if nP:
                Sc, ScB = nsbuf()
                if s_mode == "zero":
                    S.op("dve", lambda en: en.memset(Sc.rearrange("p k v -> p (k v)"), 0.0), writes=[ScB])
                else:
                    S.dma("sp", Sc, sscr[h * 256:(h + 1) * 256, :].rearrange("(k p) v -> p k v", p=128),
                          reads=[sscrB[h]], writes=[ScB])
                    if s_mode == "flag":
                        S.op("dve", lambda en: en.tensor_scalar(out=Sc.rearrange("p k v -> p (k v)"),
                                                                 in0=Sc.rearrange("p k v -> p (k v)"),
                                                                 scalar1=flag[:, 0:1], scalar2=None, op0=ALU.mult),
                             reads=[flB, ScB], writes=[ScB])
            def store_state(h=h):
                S.dma("sp", sscr[h * 256:(h + 1) * 256, :].rearrange("(k p) v -> p k v", p=128), Sc,
                      reads=[ScB], writes=[sscrB[h]])
                if last:
                    S.dma("sp", retp[h * 256:(h + 1) * 256, :].rearrange("(k p) v -> p k v", p=128), Sc,
                          reads=[ScB], is_out=True)

            for tc in range(ntc):
                is_s = tc >= nP
                if state_only:
                    cdec = float(gam ** 128)
                    for kc in range(2):
                        bk, bB = nbank()
                        mm_group(bk[:, :], bB, [(ktm[:, tc, kc * 128:(kc + 1) * 128], vtm(tc))], [aB[9], vtmB(tc)])
                        S.op("dve", lambda en, kc=kc, bk=bk: en.scalar_tensor_tensor(
                            out=Sc[:, kc, :], in0=Sc[:, kc, :], scalar=cdec, in1=bk[:, :],
                            op0=ALU.mult, op1=ALU.add), reads=[bB, ScB], writes=[ScB])
                    if tc == nP - 1:
                        store_state()
                    continue
                bk, bB = nbank()
                mm_group(bk[:, 0:128], bB, [(kT(kc, tc * 128, 128), qdT(kc, tc * 128, 128)) for kc in range(2)],
                         [aB[8], aB[6]])
                PT = arena_b[:, 13, 0:128]
                mi = 1 if is_s else 0
                S.op("dve", lambda en, bk=bk, mi=mi: en.tensor_tensor(out=PT, in0=bk[:, 0:128], in1=mk_sb[:, mi, :], op=ALU.mult),
                     reads=[bB, mkB[mi]], writes=[aB[13]])
                ob_, obB = nbank()
                if not is_s:
                    Sb = arena_b[:, 12, :].rearrange("p (k v) -> p k v", k=2)
                    S.op("act", lambda en: en.copy(out=Sb, in_=Sc), reads=[ScB], writes=[aB[12]])
                    for vc in range(4):
                        pairs = [(vtm(tc)[:, vc * 128:(vc + 1) * 128], PT)]
                        pairs += [(Sb[:, kc, vc * 128:(vc + 1) * 128], qdT(kc, tc * 128, 128)) for kc in range(2)]
                        mm_group(ob_[:, vc * 128:(vc + 1) * 128], obB, pairs, [vtmB(tc), aB[13], aB[12], aB[6]])
                else:
                    for vc in range(4):
                        mm_group(ob_[:, vc * 128:(vc + 1) * 128], obB, [(vtm(tc)[:, vc * 128:(vc + 1) * 128], PT)],
                                 [vtmB(tc), aB[13]], start=(vc == 0), stop=False)
                    cdec_s = float(gam ** 8)
                    PF = 3
                    pend = []

                    def s0_load(sq, h=h):
                        rr = (sq * NH + h) * 256
                        buf, bufB = nsbuf()
                        S.dma("sp", buf, sret[rr:rr + 256, :].rearrange("(k p) v -> p k v", p=128), writes=[bufB])
                        pend.append((buf, bufB))
                    for sq in range(min(PF, NS)):
                        s0_load(sq)
                    for s_ in range(NS):
                        r0 = (s_ * NH + h) * 256
                        S0f, S0fB = pend.pop(0)
                        S0b = arena_b[:, 7, :].rearrange("p (k v) -> p k v", k=2)
                        S.op("act", lambda en, S0f=S0f: en.copy(out=S0b, in_=S0f), reads=[S0fB], writes=[aB[7]])
                        for vc in range(4):
                            mm_group(ob_[:, vc * 128 + s_ * 8:vc * 128 + s_ * 8 + 8], obB,
                                     [(S0b[:, kc, vc * 128:(vc + 1) * 128], qdT(kc, tc * 128 + s_ * 8, 8)) for kc in range(2)],
                                     [aB[7], aB[6]], start=False, stop=(s_ == NS - 1))
                        km = arena_b[:, 12, 0:256]
                        S.op("dve", lambda en, s_=s_, tc=tc: en.tensor_scalar(out=km, in0=ktm[:, tc, :], scalar1=seqm[:, s_:s_ + 1],
                                                                               scalar2=None, op0=ALU.mult),
                             reads=[aB[9], sqB], writes=[aB[12]])
                        for kc in range(2):
                            bk, bB = nbank(avoid=ob_)
                            mm_group(bk[:, :], bB, [(km[:, kc * 128:(kc + 1) * 128], vtm(tc))], [aB[12], vtmB(tc)])
                            S.op("dve", lambda en, kc=kc, bk=bk, S0f=S0f: en.scalar_tensor_tensor(
                                out=S0f[:, kc, :], in0=S0f[:, kc, :], scalar=cdec_s, in1=bk[:, :],
                                op0=ALU.mult, op1=ALU.add), reads=[bB, S0fB], writes=[S0fB])
                        S.dma("sp", rets[r0:r0 + 256, :].rearrange("(k p) v -> p k v", p=128), S0f, reads=[S0fB], is_out=True)
                        if s_ + PF < NS:
                            s0_load(s_ + PF)
                e = alt()
                dst = o_sb[:, :, tc * 128:(tc + 1) * 128]
                srcv = ob_[:, :].rearrange("p (v t) -> p v t", v=4)
                if e == "act":
                    S.op("act", lambda en: en.copy(out=dst, in_=srcv), reads=[obB], writes=aB[0:4])
                else:
                    S.op("dve", lambda en: en.tensor_copy(out=dst, in_=srcv), reads=[obB], writes=aB[0:4])
                if not is_s:
                    cdec = float(gam ** 128)
                    for kc in range(2):
                        bk, bB = nbank()
                        mm_group(bk[:, :], bB, [(ktm[:, tc, kc * 128:(kc + 1) * 128], vtm(tc))], [aB[9], vtmB(tc)])
                        S.op("dve", lambda en, kc=kc, bk=bk: en.scalar_tensor_tensor(
                            out=Sc[:, kc, :], in0=Sc[:, kc, :], scalar=cdec, in1=bk[:, :],
                            op0=ALU.mult, op1=ALU.add), reads=[bB, ScB], writes=[ScB])
                    if tc == nP - 1:
                        store_state()
            if state_only:
                continue
            rs, rsB = rstd_of(lambda vc: o_sb[:, vc, 0:T], 4, aB[0:4], T, 1.0 / 512.0, sq_slot=12, rs_slot=5)
            for vc in range(4):
                S.op("dve", lambda en, vc=vc: en.scalar_tensor_tensor(
                    out=o_sb[:, vc, 0:T], in0=o_sb[:, vc, 0:T], scalar=vcol(V_RN + h * 4 + vc), in1=rs,
                    op0=ALU.mult, op1=ALU.mult), reads=[aB[vc], rsB, vB], writes=[aB[vc]])
            for s_ in range(2):
                def c_gr(j, bk, bB, s_=s_, h=h):
                    vc = s_ * 2 + j
                    sg = arena[:, 4, 0:T]
                    S.op("act", lambda en: en.activation(out=sg, in_=bk[:, 0:T], func=AF.Silu,
                                                          bias=vcol(V_BIN + C_GR // 128 + h * 4 + vc), scale=1.0),
                         reads=[bB, vB], writes=[aB[4]])
                    S.op("dve", lambda en: en.tensor_tensor(out=scr[:, h * 4 + vc, 0:T], in0=sg, in1=o_sb[:, vc, 0:T], op=ALU.mult),
                         reads=[aB[4], aB[vc]], writes=[sB[h * 4 + vc]])
                fm_proj(W["w_in"], C_GR + h * 512 + s_ * 256, u_rhs(T), uB, 16, T, c_gr)
        for og in range(0 if state_only else 8):
            pb_ = {}

            def c_pb(j, bk, bB):
                pb_[j] = (bk, bB)
            fm_proj(W["proj_b"], og * 256, lambda kc: scr[:, kc, 0:T], sB, 32, T, c_pb)

            def c_gtb(j, bk, bB, og=og):
                oc = og * 2 + j
                sl_ = 10 + (oc % 2)
                sg = arena[:, sl_, 0:T]
                S.op("act", lambda en: en.activation(out=sg, in_=bk[:, 0:T], func=AF.Sigmoid,
                                                      bias=vcol(V_BIN + C_GTB // 128 + oc), scale=1.0),
                     reads=[bB, vB], writes=[aB[sl_]])
                S.op("dve", lambda en: en.tensor_tensor(out=sg, in0=sg, in1=pb_[j][0][:, 0:T], op=ALU.mult),
                     reads=[aB[sl_], pb_[j][1]], writes=[aB[sl_]])
                S.op("dve", lambda en: en.tensor_tensor(out=scr[:, 32 + oc, 0:T], in0=sg, in1=scr[:, 32 + oc, 0:T], op=ALU.add),
                     reads=[aB[sl_], sB[32 + oc]], writes=[sB[32 + oc]])
            fm_proj(W["w_in"], C_GTB + og * 256, u_rhs(T), uB, 16, T, c_gtb)
        for og in range(0 if state_only else 8):
            def c_wo(j, bk, bB, og=og):
                oc = og * 2 + j
                S.op("dve", lambda en: en.tensor_tensor(out=x_fm[:, oc, 0:T], in0=x_fm[:, oc, 0:T], in1=bk[:, 0:T], op=ALU.add),
                     reads=[bB, xB[oc]], writes=[xB[oc]])
            fm_proj(W["w_out"], og * 256, lambda kc: scr[:, 32 + kc, 0:T], sB[32:48], 16, T, c_wo)

    def ple_and_out(psrcs, ydsts):
        T = 128 * len(psrcs)
        rmsnorm_u(V_PLE, T)
        for tc, psrc in enumerate(psrcs):
            pin = arena[:, 8, 0:256]
            S.dma("sp", pin, psrc, writes=[aB[8]])
            bk, bB = nbank()
            transposes([bk[:, j * 128:(j + 1) * 128] for j in range(2)],
                       [arena[:, 8, j * 128:(j + 1) * 128] for j in range(2)], identf, idB, [aB[8]], bB)
            S.op("act", lambda en, tc=tc, bk=bk: en.copy(
                out=arena_b[:, 9, :].rearrange("p (k t) -> p k t", k=2)[:, :, tc * 128:(tc + 1) * 128],
                in_=bk[:, 0:256].rearrange("p (k t) -> p k t", k=2)), reads=[bB], writes=[aB[9]])
        for og in range(8):
            pb_ = {}

            def c_pp(j, bk, bB):
                pb_[j] = (bk, bB)
            fm_proj(W["ple_proj"], og * 256, lambda kc: arena_b[:, 9, kc * 512:kc * 512 + T], [aB[9], aB[9]], 2, T, c_pp)

            def c_pg(j, bk, bB, og=og):
                oc = og * 2 + j
                sl_ = 10 + (oc % 2)
                sg = arena[:, sl_, 0:T]
                S.op("act", lambda en: en.activation(out=sg, in_=bk[:, 0:T], func=AF.Sigmoid, bias=vcol(V_PBG + oc), scale=1.0),
                     reads=[bB, vB], writes=[aB[sl_]])
                S.op("dve", lambda en: en.tensor_tensor(out=sg, in0=sg, in1=pb_[j][0][:, 0:T], op=ALU.mult),
                     reads=[aB[sl_], pb_[j][1]], writes=[aB[sl_]])
                S.op("dve", lambda en: en.tensor_tensor(out=x_fm[:, oc, 0:T], in0=x_fm[:, oc, 0:T], in1=sg, op=ALU.add),
                     reads=[aB[sl_], xB[oc]], writes=[xB[oc]])
            fm_proj(W["ple_wg"], og * 256, u_rhs(T), uB, 16, T, c_pg)
        rs, rsB = rstd_of(lambda kc: x_fm[:, kc, 0:T], 16, xB, T, 1.0 / DM)
        for kc in range(16):
            S.op("dve", lambda en, kc=kc: en.scalar_tensor_tensor(
                out=x_fm[:, kc, 0:T], in0=x_fm[:, kc, 0:T], scalar=vcol(V_FIN + kc), in1=rs,
                op0=ALU.mult, op1=ALU.mult), reads=[xB[kc], rsB, vB], writes=[xB[kc]])
        for tc, ydst in enumerate(ydsts):
            a0 = 4 * (tc % 2)
            for g in range(4):
                bk, bB = nbank()
                transposes([bk[:, j * 128:(j + 1) * 128] for j in range(4)],
                           [x_fm[:, 4 * g + j, tc * 128:(tc + 1) * 128] for j in range(4)], identf, idB,
                           xB[4 * g:4 * g + 4], bB)
                e = alt()
                if e == "act":
                    S.op("act", lambda en, g=g, bk=bk: en.copy(out=arena[:, a0 + g, :], in_=bk[:, :]), reads=[bB], writes=[aB[a0 + g]])
                else:
                    S.op("dve", lambda en, g=g, bk=bk: en.tensor_copy(out=arena[:, a0 + g, :], in_=bk[:, :]), reads=[bB], writes=[aB[a0 + g]])
            S.dma("sp", ydst.rearrange("t (a b) -> t a b", a=4),
                  arena[:, a0:a0 + 4, :], reads=aB[a0:a0 + 4], is_out=True)

    def rows(d, r0, n=128):
        return d[r0:r0 + n, :]

    for ti in range(2):
        load_x([rows(xq, ti * 512 + k * 128) for k in range(4)])
        ffn("ffn1_wg", "ffn1_wu", "ffn1_wd", V_FFN1, 512)
        mixer(4, False, state_only=True, first_mode=("always" if ti == 0 else None), s_mode=("zero" if ti == 0 else None),
              cs=[(0, 512, cosq_d[:, ti * 512:(ti + 1) * 512], sinq_d[:, ti * 512:(ti + 1) * 512])])
    S.op("dve", lambda en: en.tensor_scalar(out=hprev[:, :], in0=hprev[:, :], scalar1=flag[:, 0:1], scalar2=None, op0=ALU.mult),
         reads=[flB, hpB], writes=[hpB])
    S.op("dve", lambda en: en.tensor_scalar(out=halo[:].rearrange("p c r -> p (c r)"), in0=halo[:].rearrange("p c r -> p (c r)"),
                                             scalar1=flag[:, 0:1], scalar2=None, op0=ALU.mult),
         reads=[flB, haloB], writes=[haloB])
    for ti in range(3):
        nP = 3 if ti < 2 else 2
        hasS = (ti == 2)
        T = 384
        r0 = ti * 384
        xsrc = [rows(xp, r0 + k * 128) for k in range(nP)] + ([rows(xs, 0)] if hasS else [])
        psrc = [rows(pp, r0 + k * 128) for k in range(nP)] + ([rows(ps, 0)] if hasS else [])
        ydst = [rows(yp, r0 + k * 128) for k in range(nP)] + ([rows(ys, 0)] if hasS else [])
        cs = [(0, nP * 128, cosp_d[:, r0:r0 + nP * 128], sinp_d[:, r0:r0 + nP * 128])]
        if hasS:
            cs.append((nP * 128, 128, coss_d[:, :], sins_d[:, :]))
        load_x(xsrc)
        ffn("ffn1_wg", "ffn1_wu", "ffn1_wd", V_FFN1, T)
        mixer(nP, hasS, first_mode=("flagA" if ti == 0 else None), last=(ti == 2), cs=cs,
              s_mode=("flag" if ti == 0 else None))
        ffn("ffn2_wg", "ffn2_wu", "ffn2_wd", V_FFN2, T)
        ple_and_out(psrc, ydst)
    S.finish()
    return nc, es


def _consts():
    f32 = np.float32
    h = np.arange(NH)
    gam = 1.0 - np.exp2(-5.0 - h)
    inv = (10000.0 ** (-(np.arange(128, dtype=f32)) / f32(128))).astype(f32)
    posp = np.arange(2048, dtype=f32)
    angp = (posp[None, :] * inv[:, None]).astype(f32)
    poss = (16384 + (np.arange(128) % 8)).astype(f32)
    angs = (poss[None, :] * inv[:, None]).astype(f32)
    c = {}
    c["identf"] = np.eye(128, dtype=f32)
    c["cos_all"] = np.cos(angp).astype(f32); c["sin_all"] = np.sin(angp).astype(f32)
    c["coss"] = np.cos(angs).astype(f32); c["sins"] = np.sin(angs).astype(f32)
    j = np.arange(128)[:, None]; i = np.arange(128)[None, :]
    mkp = np.zeros((NH, 128, 128)); mks = np.zeros((NH, 128, 128))
    cdp = np.zeros((NH, 128, 512)); cds = np.zeros((NH, 128, 128))
    sdec = np.zeros((128, 16))
    for hh in range(NH):
        g = gam[hh]
        mkp[hh] = np.where(i >= j, g ** (-(j + 1.0)), 0.0) / 16.0
        tj = j % 8; ti = i % 8
        mks[hh] = np.where((j // 8 == i // 8) & (ti >= tj), g ** (-(tj + 1.0)), 0.0) / 16.0
        cdp[hh] = (g ** ((np.arange(512) % 128) + 1.0))[None, :]
        cds[hh] = (g ** ((np.arange(128) % 8) + 1.0))[None, :]
        sdec[:, hh] = g ** (127.0 - np.arange(128)) / 16.0
        sdec[:, 8 + hh] = g ** (7.0 - (np.arange(128) % 8)) / 16.0
    c["mkp"] = mkp.reshape(NH * 128, 128).astype(f32); c["mks"] = mks.reshape(NH * 128, 128).astype(f32)
    c["cdp"] = cdp.reshape(NH * 128, 512).astype(f32); c["cds"] = cds.reshape(NH * 128, 128).astype(f32)
    c["sdec"] = sdec.astype(f32)
    c["seqm"] = (np.arange(128)[:, None] // 8 == np.arange(16)[None, :]).astype(f32)
    return c


def _fm(v):
    return np.ascontiguousarray(np.asarray(v, dtype=np.float32).reshape(-1, 128).T)


_CACHE = {}


def kernel(**inp):
    f32 = np.float32
    g = {k: np.asarray(v) for k, v in inp.items()}
    vecs = np.concatenate([
        _fm(g["ffn1_norm"][0]), _fm(g["mix_norm"][0]), _fm(g["ffn2_norm"][0]), _fm(g["ple_norm"][0]),
        _fm(g["final_norm"]), _fm(g["b_in"][0]),
        _fm(g["conv_w"][0][0]), _fm(g["conv_w"][0][1]), _fm(g["conv_w"][0][2]), _fm(g["conv_w"][0][3]),
        _fm(g["conv_b"][0]), _fm(g["lru_ba"][0]), _fm(g["lru_bx"][0]), _fm(g["lru_lambda"][0]),
        _fm(g["ret_norm"][0].reshape(-1)), _fm(g["ple_bg"][0])], axis=1).astype(f32)
    assert vecs.shape == (128, NV), vecs.shape
    shared = dict(_consts())
    cos_all = shared.pop("cos_all"); sin_all = shared.pop("sin_all")
    shared["cosq"] = np.ascontiguousarray(cos_all[:, 0:1024]); shared["sinq"] = np.ascontiguousarray(sin_all[:, 0:1024])
    shared["vecs"] = vecs
    shared["bv"] = np.ascontiguousarray(g["b_in"][0][C_V:C_V + 4096].reshape(1, 4096))
    for nm in ("ffn1_wg", "ffn1_wu", "ffn1_wd", "w_in", "proj_a", "proj_b", "w_out",
               "ffn2_wg", "ffn2_wu", "ffn2_wd", "ple_wg", "ple_proj"):
        shared[nm] = g[nm][0]
    shared["lru_wa"] = g["lru_wa"][0].reshape(NH * 256, 256)
    shared["lru_wx"] = g["lru_wx"][0].reshape(NH * 256, 256)
    in_maps = []
    for c in range(NCORES):
        b = c % 4
        hf = c // 4
        m = dict(shared)
        m["xp"] = g["x_prompt"][b, hf * 1024:(hf + 1) * 1024]
        m["xq"] = g["x_prompt"][b, 0:1024]
        m["cosp"] = np.ascontiguousarray(cos_all[:, hf * 1024:(hf + 1) * 1024])
        m["sinp"] = np.ascontiguousarray(sin_all[:, hf * 1024:(hf + 1) * 1024])
        fl = np.zeros((128, 2), np.float32); fl[:, 0] = float(hf); fl[:, 1] = 1.0 - float(hf)
        m["flag"] = fl
        m["xs"] = g["x_sample"][16 * c:16 * c + 16].reshape(TS, DM)
        m["pp"] = g["p_prompt"][0, b, hf * 1024:(hf + 1) * 1024]
        m["ps"] = g["p_sample"][0, 16 * c:16 * c + 16].reshape(TS, 256)
        m["slru"] = g["state_lru"][0, 16 * c:16 * c + 16]
        m["sconv"] = g["state_conv"][0, 16 * c:16 * c + 16].reshape(48, DM)
        m["sret"] = g["state_ret"][0, 16 * c:16 * c + 16].reshape(16 * NH * 256, 512)
        in_maps.append(m)
    if "nc" not in _CACHE:
        _CACHE["nc"] = build_nc()
    nc, _es = _CACHE["nc"]
    res = run_bass_kernel_spmd(nc, in_maps, core_ids=list(range(NCORES)))
    R = res.results
    y_prompt = np.stack([np.concatenate([R[b]["yp"], R[b + 4]["yp"]]) for b in range(4)]).astype(f32)
    y_sample = np.concatenate([R[c]["ys"].reshape(16, 8, DM) for c in range(NCORES)]).astype(f32)
    lru_p = np.stack([R[b + 4]["lrup"].reshape(DM) for b in range(4)])[None].astype(f32)
    conv_p = np.stack([R[b + 4]["convp"] for b in range(4)])[None].astype(f32)
    ret_p = np.stack([R[b + 4]["retp"].reshape(NH, 256, 512) for b in range(4)])[None].astype(f32)
    lru_s = np.concatenate([R[c]["lrus"] for c in range(NCORES)])[None].astype(f32)
    conv_s = np.concatenate([R[c]["convs"].reshape(16, 3, DM) for c in range(NCORES)])[None].astype(f32)
    ret_s = np.concatenate([R[c]["rets"].reshape(16, NH, 256, 512) for c in range(NCORES)])[None].astype(f32)
    return (y_prompt, y_sample, lru_p, conv_p, ret_p, lru_s, conv_s, ret_s)
```
